# Optimizing a Trainium2 kernel written in Bass

```python
import jax, jax.numpy as jnp
from jax import lax
import numpy as np

D_MODEL = 1024
BATCH = 8
SEQ = 2048
DEPTH = 2
DEC_BATCH = 128
DEC_SEQ = 1
PAST_LEN = 16384
PAGE_SIZE = 128

N_EVEN = (DEPTH + 1) // 2
N_ODD = DEPTH // 2
D_POOL = D_MODEL // 2
POOL_WINDOWS = (2, 4, 8, 16)
N_POOL_GROUPS = len(POOL_WINDOWS)
POOL_GROUP = D_POOL // N_POOL_GROUPS
POOL_HIST = max(POOL_WINDOWS) - 1
D_CONV = D_MODEL // 2
CONV_WIDTH = 3
CONV_HIST = CONV_WIDTH - 1
D_IN_EVEN = D_POOL + 3 * D_CONV
D_MIX_EVEN = D_POOL + D_CONV
D_GATE = D_MODEL
CHUNK = 128
N_SG_HEADS = 8
SG_HEAD = D_GATE // N_SG_HEADS
D_FF = -(-8 * D_MODEL // (3 * 256)) * 256
EPS = 1e-6

kernel_name = "hybrid_pool_shortconv_chunkgmlp_decode_step"


def rmsnorm(x, g):
    xf = x.astype(jnp.float32)
    y = xf * lax.rsqrt(jnp.mean(xf * xf, axis=-1, keepdims=True) + EPS)
    return (y * g.astype(jnp.float32)).astype(x.dtype)


def swiglu(h, w_gate, w_up, w_down):
    return (jax.nn.silu(h @ w_gate) * (h @ w_up)) @ w_down


def pool_mixer(hist, p, start, w_pool, scale):
    B, T, _ = p.shape
    full = jnp.concatenate([hist, p], axis=1)
    cs = jnp.cumsum(full.astype(jnp.float32), axis=1)
    cs = jnp.pad(cs, ((0, 0), (1, 0), (0, 0)))
    pos = start + jnp.arange(T)
    outs = []
    for g, w in enumerate(POOL_WINDOWS):
        sl = slice(g * POOL_GROUP, (g + 1) * POOL_GROUP)
        hi = cs[:, POOL_HIST + 1:POOL_HIST + 1 + T, sl]
        lo = cs[:, POOL_HIST + 1 - w:POOL_HIST + 1 - w + T, sl]
        cnt = jnp.minimum(pos + 1, w).astype(jnp.float32)[None, :, None]
        outs.append((hi - lo) / cnt)
    pooled = jnp.stack(outs, axis=2)
    pg = p.reshape(B, T, N_POOL_GROUPS, POOL_GROUP).astype(jnp.float32)
    d = (pooled - pg).astype(p.dtype)
    mixed = jnp.einsum('btgc,gcd->btgd', d, w_pool).reshape(B, T, D_POOL)
    return mixed * scale, full[:, -POOL_HIST:]


def conv_mixer(hist, xb, b_gate, c_gate, conv_w):
    T = xb.shape[1]
    full = jnp.concatenate([hist, c_gate * xb], axis=1)
    y = full[:, 0:T] * conv_w[0]
    for k in range(1, CONV_WIDTH):
        y = y + full[:, k:k + T] * conv_w[k]
    return b_gate * y, full[:, -CONV_HIST:]


def chunk_gating(z, g_v, w_s, b_s):
    B, T, _ = z.shape
    u, v = z[..., :D_GATE], z[..., D_GATE:]
    v = rmsnorm(v, g_v)
    Tp = -(-T // CHUNK) * CHUNK
    vp = jnp.pad(v, ((0, 0), (0, Tp - T), (0, 0))).reshape(B, Tp // CHUNK, CHUNK, N_SG_HEADS, SG_HEAD)
    mask = jnp.tril(jnp.ones((CHUNK, CHUNK), dtype=bool))
    ws = jnp.where(mask[None], w_s, jnp.zeros((), w_s.dtype))
    mixed = jnp.einsum('hts,bnshd->bnthd', ws, vp) + b_s.T[None, None, :, :, None]
    mixed = mixed.reshape(B, Tp, D_GATE)[:, :T]
    return u * mixed, v


def run_group(x, start, pool_hist, conv_hist, norm_mix, norm_ffn, norm_final,
              w_in_even, w_pool, pool_scale, conv_w, w_out_even,
              w_in_odd, norm_sg, w_s, b_s, w_out_odd,
              ffn_w_gate, ffn_w_up, ffn_w_down):
    pool_new, conv_new, v_new = [], [], []
    for layer in range(DEPTH):
        h = rmsnorm(x, norm_mix[layer])
        if layer % 2 == 0:
            e = layer // 2
            z = h @ w_in_even[e]
            p, xb, bg, cg = jnp.split(z, [D_POOL, D_POOL + D_CONV, D_POOL + 2 * D_CONV], axis=-1)
            a_out, ph = pool_mixer(pool_hist[e], p, start, w_pool[e], pool_scale[e])
            b_out, ch = conv_mixer(conv_hist[e], xb, bg, cg, conv_w[e])
            mix = jnp.concatenate([a_out, b_out], axis=-1) @ w_out_even[e]
            pool_new.append(ph)
            conv_new.append(ch)
        else:
            o = layer // 2
            z = jax.nn.gelu(h @ w_in_odd[o], approximate=False)
            c_out, v = chunk_gating(z, norm_sg[o], w_s[o], b_s[o])
            mix = c_out @ w_out_odd[o]
            v_new.append(v)
        x = x + mix
        h = rmsnorm(x, norm_ffn[layer])
        x = x + swiglu(h, ffn_w_gate[layer], ffn_w_up[layer], ffn_w_down[layer])
    return rmsnorm(x, norm_final), jnp.stack(pool_new), jnp.stack(conv_new), jnp.stack(v_new)


def setup_inputs(seed: int = 0) -> dict:
    key = jax.random.key(seed)
    ks = jax.random.split(key, 24)
    nrm = lambda k, shape, s: jax.random.normal(k, shape, jnp.float32) * s
    return {
        "x_prompt": nrm(ks[0], (BATCH, SEQ, D_MODEL), 1.0),
        "x_sample": nrm(ks[1], (DEC_BATCH, DEC_SEQ, D_MODEL), 1.0),
        "state_pool": nrm(ks[2], (N_EVEN, DEC_BATCH, POOL_HIST, D_POOL), 1.0),
        "state_conv": nrm(ks[3], (N_EVEN, DEC_BATCH, CONV_HIST, D_CONV), 1.0),
        "norm_mix": 1.0 + nrm(ks[4], (DEPTH, D_MODEL), 0.05),
        "norm_ffn": 1.0 + nrm(ks[5], (DEPTH, D_MODEL), 0.05),
        "norm_final": 1.0 + nrm(ks[6], (D_MODEL,), 0.05),
        "w_in_even": nrm(ks[7], (N_EVEN, D_MODEL, D_IN_EVEN), D_MODEL ** -0.5),
        "w_pool": nrm(ks[8], (N_EVEN, N_POOL_GROUPS, POOL_GROUP, POOL_GROUP), POOL_GROUP ** -0.5),
        "pool_scale": 1.0 + nrm(ks[9], (N_EVEN, D_POOL), 0.1),
        "conv_w": nrm(ks[10], (N_EVEN, CONV_WIDTH, D_CONV), CONV_WIDTH ** -0.5),
        "w_out_even": nrm(ks[11], (N_EVEN, D_MIX_EVEN, D_MODEL), D_MIX_EVEN ** -0.5),
        "w_in_odd": nrm(ks[12], (N_ODD, D_MODEL, 2 * D_GATE), D_MODEL ** -0.5),
        "norm_sg": 1.0 + nrm(ks[13], (N_ODD, D_GATE), 0.05),
        "w_s": nrm(ks[14], (N_ODD, N_SG_HEADS, CHUNK, CHUNK), CHUNK ** -0.5),
        "b_s": 1.0 + nrm(ks[15], (N_ODD, N_SG_HEADS, CHUNK), 0.1),
        "w_out_odd": nrm(ks[16], (N_ODD, D_GATE, D_MODEL), D_GATE ** -0.5),
        "ffn_w_gate": nrm(ks[17], (DEPTH, D_MODEL, D_FF), D_MODEL ** -0.5),
        "ffn_w_up": nrm(ks[18], (DEPTH, D_MODEL, D_FF), D_MODEL ** -0.5),
        "ffn_w_down": nrm(ks[19], (DEPTH, D_FF, D_MODEL), D_FF ** -0.5),
    }


def reference(x_prompt, x_sample, state_pool, state_conv, norm_mix, norm_ffn, norm_final,
              w_in_even, w_pool, pool_scale, conv_w, w_out_even,
              w_in_odd, norm_sg, w_s, b_s, w_out_odd,
              ffn_w_gate, ffn_w_up, ffn_w_down):
    weights = (norm_mix, norm_ffn, norm_final, w_in_even, w_pool, pool_scale, conv_w, w_out_even,
               w_in_odd, norm_sg, w_s, b_s, w_out_odd, ffn_w_gate, ffn_w_up, ffn_w_down)
    pool0 = jnp.zeros((N_EVEN, BATCH, POOL_HIST, D_POOL), x_prompt.dtype)
    conv0 = jnp.zeros((N_EVEN, BATCH, CONV_HIST, D_CONV), x_prompt.dtype)
    y_prompt, pool_p, conv_p, _v_p = run_group(x_prompt, 0, pool0, conv0, *weights)
    y_sample, pool_s, conv_s, v_s = run_group(x_sample, PAST_LEN, state_pool, state_conv, *weights)
    return (y_prompt, y_sample, pool_p, pool_s, conv_p, conv_s, v_s)
```

```python
import contextlib
import numpy as np
import concourse.bass as bass
import concourse.mybir as mybir
from concourse.bass_utils import run_bass_kernel_spmd

F32 = mybir.dt.float32
BF16 = mybir.dt.bfloat16
ALU = mybir.AluOpType
AF = mybir.ActivationFunctionType
AX = mybir.AxisListType

ENGS = ("pe", "act", "dve", "pool", "sp")

NCORES = 8
D = 1024
KC = 8
SEQ = 2048
NS = 16
T = SEQ + NS
TT = [(0, 512), (512, 512), (1024, 512), (1536, 256), (1792, 272)]
LASTP = (1792, 256)
XOFF = [KC * sum(w_ for _, w_ in TT[:n_]) for n_ in range(len(TT))]
DFF = 2816
NJ = DFF // 128
EPS = 1e-6
WINS = (2, 4, 8, 16)
WSLOT = 4096
NWSLOT = 3
FUSE_L0 = True
LAG = 1
DEBUG = False
SELF_SYNC = True


class Buf:
    __slots__ = ("w", "r", "name", "const")

    def __init__(self, name="", const=False):
        self.w = None
        self.r = {}
        self.name = name
        self.const = const


class Sched:
    def __init__(self, nc, stack, self_sync=True):
        self.nc = nc
        self.stack = stack
        self.ops = {e: [] for e in ENGS}
        self.sems = {}
        self.cnt = {}
        self.seen = {e: {} for e in ENGS}
        self.self_sync = self_sync
        for e in ENGS:
            self._sem(e)

    def _sem(self, key):
        if key not in self.sems:
            self.sems[key] = self.stack.enter_context(self.nc.semaphore("s_" + str(key)))
            self.cnt[key] = 0
        return self.sems[key]

    def _waits(self, eng, reads, writes, extra=()):
        deps = {}

        def add(ev):
            if ev is None:
                return
            k, v = ev
            if deps.get(k, 0) < v:
                deps[k] = v

        for b in reads:
            add(b.w)
        for b in writes:
            add(b.w)
            for k, v in b.r.items():
                add((k, v))
        for ev in extra:
            add(ev)
        waits = []
        seen = self.seen[eng]
        for k, v in deps.items():
            if k == eng and (eng == "pe" or not self.self_sync):
                continue
            if seen.get(k, 0) < v:
                seen[k] = v
                waits.append((k, v))
        return waits

    def _record(self, ev, reads, writes):
        for b in writes:
            b.w = ev
            b.r = {}
        k, v = ev
        for b in reads:
            if b.const:
                continue
            if b.r.get(k, 0) < v:
                b.r[k] = v

    def op(self, eng, fn, reads=(), writes=(), extra=()):
        waits = self._waits(eng, reads, writes, extra)
        self.cnt[eng] += 1
        ev = (eng, self.cnt[eng])
        sems = self.sems

        def emit(E, waits=waits, fn=fn, sem=sems[eng]):
            for k, v in waits:
                E.wait_ge(sems[k], v)
            ins = fn(E)
            ins.then_inc(sem, 1)

        self.ops[eng].append(emit)
        self._record(ev, reads, writes)
        return ev

    def dma(self, eng, out, in_, reads=(), writes=(), sem="dma", extra=(), **kw):
        waits = self._waits(eng, reads, writes, extra)
        s = self._sem(sem)
        self.cnt[sem] += 16
        ev = (sem, self.cnt[sem])
        sems = self.sems

        def emit(E, waits=waits, s=s):
            for k, v in waits:
                E.wait_ge(sems[k], v)
            E.dma_start(out=out, in_=in_, **kw).then_inc(s, 16)

        self.ops[eng].append(emit)
        self._record(ev, reads, writes)
        return ev

    def wait(self, eng, events):
        waits = self._waits(eng, (), (), extra=events)
        sems = self.sems

        def emit(E, waits=waits):
            for k, v in waits:
                E.wait_ge(sems[k], v)

        self.ops[eng].append(emit)

    def emit_all(self):
        nc = self.nc
        with nc.Block() as block:
            @block.tensor
            def _(E):
                for f in self.ops["pe"]:
                    f(E)

            @block.scalar
            def _(E):
                for f in self.ops["act"]:
                    f(E)

            @block.vector
            def _(E):
                for f in self.ops["dve"]:
                    f(E)

            @block.gpsimd
            def _(E):
                for f in self.ops["pool"]:
                    f(E)

            @block.sync
            def _(E):
                for f in self.ops["sp"]:
                    f(E)


def ffn_groups():
    sizes = (4, 4, 4, 4, 2, 4)
    gs = []
    j = 0
    for n in sizes:
        gs.append(list(range(j, j + n)))
        j += n
    assert j == NJ
    return gs


def weight_plan():
    plan = []
    plan.append(("in0_pool", KC * 512))
    for j in range(4):
        plan.append((f"in0_conv{j}", KC * 384))
    for h in range(2):
        plan.append((f"out0_{h}", KC * 512))

    def ffn(l):
        gs = ffn_groups()
        def up(g):
            for jj in sorted(set(j // 2 for j in gs[g])):
                plan.append((f"gu{l}_{jj}", 2 * KC * 256))

        def dn(g):
            plan.append((f"dn{l}_{g}", len(gs[g]) * 1024))
        G = len(gs)
        up(0)
        for g in range(1, G):
            up(g)
            dn(g - 1)
        dn(G - 1)
    ffn(0)
    for h in range(2):
        plan.append((f"v_{h}", KC * 512))
    for h in range(2):
        plan.append((f"u_{h}", KC * 512))
    for h in range(2):
        plan.append((f"out1_{h}", KC * 512))
    ffn(1)
    offs = {}
    o = 0
    for name, n in plan:
        offs[name] = (o, n)
        o += n
    return plan, offs, o


def _kmajor(w):
    C = w.shape[1]
    return w.reshape(KC, 128, C).transpose(1, 0, 2)


def pack_weights(inp):
    plan, offs, tot = weight_plan()
    out = np.empty((128, tot), np.float32)

    def put(name, arr):
        o, n = offs[name]
        a = np.ascontiguousarray(arr).reshape(128, -1)
        assert a.shape[1] == n, (name, a.shape, n)
        out[:, o:o + n] = a

    wi = inp["w_in_even"][0]
    put("in0_pool", _kmajor(wi[:, 0:512]))
    for j in range(4):
        cols = np.concatenate([np.arange(512 + j * 128, 512 + (j + 1) * 128),
                               np.arange(1536 + j * 128, 1536 + (j + 1) * 128),
                               np.arange(1024 + j * 128, 1024 + (j + 1) * 128)])
        put(f"in0_conv{j}", _kmajor(wi[:, cols]))
    wo = inp["w_out_even"][0]
    for h in range(2):
        put(f"out0_{h}", _kmajor(wo[:, h * 512:(h + 1) * 512]))
    gs = ffn_groups()
    for l in range(2):
        wg, wu, wd = inp["ffn_w_gate"][l], inp["ffn_w_up"][l], inp["ffn_w_down"][l]
        for jj in range(NJ // 2):
            g_ = _kmajor(wg[:, jj * 256:(jj + 1) * 256])
            u_ = _kmajor(wu[:, jj * 256:(jj + 1) * 256])
            put(f"gu{l}_{jj}", np.stack([g_, u_], axis=1))
        for g, chunks in enumerate(gs):
            r0 = chunks[0] * 128
            nch = len(chunks)
            blk = wd[r0:r0 + nch * 128, :].reshape(nch, 128, D).transpose(1, 0, 2)
            put(f"dn{l}_{g}", blk)
    wio = inp["w_in_odd"][0]
    for h in range(2):
        put(f"v_{h}", _kmajor(wio[:, 1024 + h * 512:1024 + (h + 1) * 512]))
    for h in range(2):
        put(f"u_{h}", _kmajor(wio[:, h * 512:(h + 1) * 512]))
    woo = inp["w_out_odd"][0]
    for h in range(2):
        put(f"out1_{h}", _kmajor(woo[:, h * 512:(h + 1) * 512]))
    return out


VC_NMIX = 0
VC_NFFN = 16
VC_NFIN = 32
VC_PSCALE = 40
VC_CONVW = 44
NVEC = 56


def pack_vecs(inp):
    v = np.empty((128, NVEC), np.float32)

    def cols(a):
        return a.reshape(-1, 128).T
    v[:, VC_NMIX:VC_NMIX + 16] = cols(inp["norm_mix"].reshape(-1))
    v[:, VC_NFFN:VC_NFFN + 16] = cols(inp["norm_ffn"].reshape(-1))
    v[:, VC_NFIN:VC_NFIN + 8] = cols(inp["norm_final"])
    v[:, VC_PSCALE:VC_PSCALE + 4] = cols(inp["pool_scale"][0])
    v[:, VC_CONVW:VC_CONVW + 12] = cols(inp["conv_w"][0].reshape(-1))
    return v


def build_program():
    plan, offs, WTOT = weight_plan()
    nc = bass.Bass("TRN2", target_bir_lowering=False)

    def din(name, shape):
        return nc.dram_tensor(name, list(shape), F32, kind="ExternalInput").ap()

    def dout(name, shape):
        return nc.dram_tensor(name, list(shape), F32, kind="ExternalOutput").ap()

    d_x = din("xT", (128, KC * T))
    d_w = din("wts", (128, WTOT))
    d_vecs = din("vecs", (128, NVEC))
    d_gv = din("gv_rep", (128, D))
    d_wpool = din("wpool", (128, 4 * 128))
    d_wsT = din("wsT", (128, 8 * 128))
    d_ws00 = din("ws00", (16, 8))
    d_bs = din("bs", (2, 8 * 128))
    d_bs0 = din("bs0", (2, 8 * 16))
    d_sel = din("sel", (2, 2))
    d_invc = din("invc", (128, 4 * 16))
    d_hp = din("hp", (128, 4 * 16 * 15))
    d_hc = din("hc", (128, 4 * 16 * 2))

    o_y = dout("yT", (128, KC * T))
    o_pp = dout("o_pp", (128, 4 * 15))
    o_ps = dout("o_ps", (128, 4 * 16 * 15))
    o_cp = dout("o_cp", (128, 4 * 2))
    o_cs = dout("o_cs", (128, 4 * 16 * 2))
    o_v = dout("o_v", (16, D))
    if DEBUG:
        o_dbg = dout("dbg", (128, 4096))

    with contextlib.ExitStack() as st:
        sc = Sched(nc, st, self_sync=SELF_SYNC)

        def sb(name, shape, dt):
            return st.enter_context(nc.sbuf_tensor(name, list(shape), dt))

        xT = sb("xTs", (128, KC * T), F32)

        def xs(c, n):
            w_ = TT[n][1]
            return xT[:, XOFF[n] + c * w_:XOFF[n] + (c + 1) * w_]

        def xrange_(n, c_lo=0, c_hi=KC):
            w_ = TT[n][1]
            return slice(XOFF[n] + c_lo * w_, XOFF[n] + c_hi * w_)
        hT = sb("hTs", (128, KC, T), BF16)
        mix = sb("mix", (128, 8, T), BF16)
        R1 = sb("R1", (128, 8704), F32)
        wring = [sb(f"wring{i}", (128, WSLOT), BF16) for i in range(NWSLOT)]
        NSQ = 8
        sqall = sb("sqall", (128, NSQ * 512), BF16)
        sq = [sqall[:, i * 512:(i + 1) * 512] for i in range(NSQ)]
        gtmp = [sb(f"gtmp{i}", (128, 512), F32) for i in range(2)]
        vecs = sb("vecs_s", (128, NVEC), F32)
        wpool = sb("wpool_s", (128, 4 * 128), BF16)
        wsT = sb("wsT_s", (128, 8 * 128), BF16)
        ws00 = sb("ws00_s", (16, 8), F32)
        ident16 = sb("ident16", (16, 16), F32)
        diag = sb("diag", (16, 8 * 16), BF16)
        bs2 = sb("bs2", (128, 8 * 128), BF16)
        bs02 = sb("bs02", (128, 8 * 16), BF16)
        sel = sb("sel_s", (2, 2), F32)
        invc = sb("invc_s", (128, 4 * 16), F32)
        opp_sb = sb("opp_sb", (128, 4 * 15), F32)
        hc = sb("hc_s", (128, 4 * 16 * 2), F32)
        ocs_sb = sb("ocs_sb", (128, 4 * 16 * 2), F32)
        ocp_sb = sb("ocp_sb", (128, 4 * 2), F32)
        hsum = sb("hsum", (128, 4 * 16), F32)
        ones = sb("ones", (128, 128), BF16)
        ssv = sb("ssv", (128, 17), F32)
        rsv = sb("rsv", (128, 17), F32)
        small = sb("small", (128, 64), F32)
        mixflat = mix[:].rearrange("p s t -> p (s t)")
        scr = mixflat[:, 4 * T:8 * T].bitcast(F32)
        hp = scr[:, 2048:3008]
        ops_sb = mixflat[:, 7 * T:7 * T + 1920].bitcast(F32)
        vtmp2 = [mixflat[:, (6 + i) * T:(6 + i) * T + 2048].bitcast(F32) for i in range(2)]
        junk = mixflat[:, 0:1024]
        gv = mixflat[:, T:T + 2048].bitcast(F32)
        vout = sqall[0:16, 0:2048].bitcast(F32)
        banks = [st.enter_context(nc.psum_tensor(f"bank{i}", [128, 512], F32)) for i in range(8)]
        print("sbuf bytes remaining:", nc.sbuf_bytes_remaining)

        PW = 16 + T
        Pb = [R1[:, 0:PW], mixflat[:, 4 * T:4 * T + 2 * PW].bitcast(F32)]
        bufB = R1[:, PW:2 * PW]
        bufC = R1[:, 2 * PW:3 * PW]
        dbuf = R1[:, 3 * PW:3 * PW + T // 2].bitcast(BF16)
        YO = 3 * PW + T // 2
        ybufs = [R1[:, YO:YO + 512], R1[:, YO + 512:YO + 1024]]
        CW = 2 + T
        xsb = R1[:, 0:T]
        cbuf = R1[:, PW:PW + CW]
        vN = R1[:, 0:17 * 512].bitcast(BF16)

        xB = [[Buf(f"x{c}_{n}") for n in range(5)] for c in range(KC)]
        hB = [[Buf(f"h{c}_{n}") for n in range(5)] for c in range(KC)]
        mB = [[Buf(f"m{c}_{n}") for n in range(5)] for c in range(8)]
        bankB = [Buf(f"bank{i}") for i in range(8)]
        wB = [Buf(f"w{i}") for i in range(NWSLOT)]
        sqB = [Buf(f"sq{i}") for i in range(NSQ)]
        gtmpB = [Buf(f"gtmp{i}") for i in range(2)]
        cst = Buf("const")
        PbB = [[Buf(f"P{i}_{n}") for n in range(5)] for i in range(2)]
        BB = [Buf(f"B_{n}") for n in range(5)]
        CB = [Buf(f"C_{n}") for n in range(5)]
        DB = [Buf(f"D_{n}") for n in range(5)]
        cXB = [Buf(f"cX_{n}") for n in range(5)]
        cCB = [Buf(f"cC_{n}") for n in range(5)]
        yB = [Buf("y0"), Buf("y1")]
        allR1 = PbB[0] + BB + CB + DB + cXB + cCB + yB

        def merge(dsts, srcs):
            for s_ in srcs:
                evs = ([s_.w] if s_.w is not None else []) + list(s_.r.items())
                for d_ in dsts:
                    for (k_, v_) in evs:
                        if d_.r.get(k_, 0) < v_:
                            d_.r[k_] = v_
        vNB = [Buf(f"vN{i}") for i in range(17)]

        st_ = {"bank": 0, "sq": 0, "rstd": 0, "gtmp": 0, "y": 0}

        def nxt(kind, n):
            i = st_[kind]
            st_[kind] = (i + 1) % n
            return i

        def nbank():
            i = nxt("bank", 8)
            return banks[i], bankB[i]

        wstate = {"issued": 0, "acq": 0, "ring_i": 0, "ring_a": 0}
        SPECIAL = ("u_0", "v_1")
        wX = mixflat[:, 4 * T:4 * T + WSLOT]

        def wXB():
            return [mB[4][n] for n in range(5)] + [mB[5][n] for n in range(5)]

        def w_issue():
            i = wstate["issued"]
            while i < len(plan) and plan[i][0] in SPECIAL:
                i += 1
            if i >= len(plan):
                wstate["issued"] = i
                return
            name, n = plan[i]
            o, _ = offs[name]
            slot = wstate["ring_i"] % NWSLOT
            wstate["ring_i"] += 1
            sc.dma("pool", wring[slot][:, 0:n], d_w[:, o:o + n], writes=[wB[slot]], sem=f"w{slot}")
            wstate["issued"] = i + 1

        wY = mixflat[:, 2 * T:2 * T + WSLOT]

        def spec_slot(name):
            if name == "u_0":
                return wX, wXB(), "wX"
            return wY, [mB[2][n] for n in range(5)] + [mB[3][n] for n in range(5)], "wY"

        def w_issue_special(name):
            o, n = offs[name]
            dst, bufs, sem_ = spec_slot(name)
            sc.dma("pool", dst[:, 0:n], d_w[:, o:o + n], writes=bufs, sem=sem_)

        def w_acquire(name):
            i = wstate["acq"]
            assert plan[i][0] == name, (plan[i][0], name)
            wstate["acq"] = i + 1
            if name in SPECIAL:
                dst, bufs, _ = spec_slot(name)
                return dst, bufs
            slot = wstate["ring_a"] % NWSLOT
            wstate["ring_a"] += 1
            return wring[slot], wB[slot]

        def w_release():
            w_issue()

        cB = {}

        def cload(name, dst, src, eng="sp"):
            cB[name] = Buf(name)
            sc.dma(eng, dst, src, writes=[cB[name]], sem=("cin" if eng == "sp" else "cinp"))

        xev = []
        for n, (c0, w) in enumerate(TT):
            if n == 0:
                for hh in range(2):
                    sc.dma("sp", xT[:, xrange_(n, 4 * hh, 4 * hh + 4)], d_x[:, xrange_(n, 4 * hh, 4 * hh + 4)],
                           writes=[xB[c][n] for c in range(4 * hh, 4 * hh + 4)], sem=f"xin0_{hh}")
                xev.append(xB[KC - 1][0].w)
                cload("vecs", vecs[:], d_vecs)
                continue
            xev.append(sc.dma("act" if n % 2 == 1 else "sp", xT[:, xrange_(n)], d_x[:, xrange_(n)],
                              writes=[xB[c][n] for c in range(KC)], sem=f"xin{n}"))
            if n == 1:
                cload("invc", invc[:], d_invc)
                cload("hp", hp, d_hp)
                cload("hc", hc[:], d_hc)
                cload("ws00", ws00[:], d_ws00)
        cin_total = ("cin", sc.cnt["cin"])
        for k_, b in cB.items():
            b.w = cin_total

        onesB = Buf("ones")
        sc.op("pool", lambda E: E.memset(ones[:], 1.0), writes=[onesB])
        epsc = small[:, 33:34]
        bsB = Buf("bs")
        sc.op("pool", lambda E: E.memset(bs2[:], 0.0), writes=[bsB])
        sc.op("pool", lambda E: E.memset(bs02[:], 0.0), writes=[bsB])
        sc.op("pool", lambda E: E.memset(epsc, EPS), writes=[onesB])
        padB = Buf("pad")
        sc.op("pool", lambda E: E.memset(R1[:, 0:3 * PW], 0.0), writes=PbB[0] + BB + CB)
        identB = Buf("ident")
        sc.op("pool", lambda E: E.memset(ident16[:], 0.0), writes=[identB])
        sc.op("pool", lambda E: E.affine_select(
            out=ident16[:], in_=ident16[:], pattern=[[-1, 16]], compare_op=ALU.not_equal,
            fill=1.0, base=0, channel_multiplier=1), reads=[identB], writes=[identB])
        cload("wpool", wpool[:], d_wpool, eng="pool")
        sc.wait("pool", [xev[0]])
        w_issue()
        sc.wait("pool", [xev[4]])
        w_issue()
        w_issue()
        hsumB = Buf("hsum")
        opsB = Buf("ops")
        ocsB = Buf("ocs")
        hp4 = hp.rearrange("p (g s r) -> p g s r", g=4, s=16)
        ops4 = ops_sb.rearrange("p (g s r) -> p g s r", g=4, s=16)
        hc4 = hc[:].rearrange("p (j s r) -> p j s r", j=4, s=16)
        ocs4 = ocs_sb[:].rearrange("p (j s r) -> p j s r", j=4, s=16)

        def early_setup():
            for g, wdw in enumerate(WINS):
                sc.op("dve", lambda E, g=g, wdw=wdw: E.tensor_reduce(
                    out=hsum[:, g * 16:(g + 1) * 16], in_=hp4[:, g, :, 15 - (wdw - 1):15],
                    axis=AX.X, op=ALU.add), reads=[cB["hp"]], writes=[hsumB])
            for g in range(4):
                sc.op("dve", lambda E, g=g: E.tensor_copy(out=ops4[:, g, :, 0:14], in_=hp4[:, g, :, 1:15]),
                      reads=[cB["hp"]], writes=[opsB])
            sc.op("dve", lambda E: E.tensor_copy(out=ocs4[:, :, :, 0:1], in_=hc4[:, :, :, 1:2]),
                  reads=[cB["hc"]], writes=[ocsB])
            merge([mB[slot][n] for slot in range(4, 8) for n in range(5)] + PbB[1], [cB["hp"]])

        wsTB = Buf("wsT")
        diagB = Buf("diag")
        wsT_f = R1[:, 0:1024]
        bs_f = R1[0:2, 1024:2048]
        bs0_f = R1[0:2, 2048:2176]
        bsA = R1[0:2, 2176:2176 + 512].bitcast(BF16)
        bsL = R1[0:2, 2688:2688 + 512].bitcast(BF16)
        bs0A = R1[0:2, 3200:3200 + 64].bitcast(BF16)
        bs0L = R1[0:2, 3264:3264 + 64].bitcast(BF16)

        def late_setup():
            lB = {}
            for name, dst, src in (("wsT_f", wsT_f, d_wsT), ("bs_f", bs_f, d_bs), ("bs0_f", bs0_f, d_bs0)):
                lB[name] = Buf(name)
                sc.dma("sp", dst, src, writes=[lB[name]] + allR1, sem="cin_" + name)
            sc.op("pool", lambda E: E.affine_select(
                out=wsT_f.rearrange("p (h t) -> p h t", h=8),
                in_=wsT_f.rearrange("p (h t) -> p h t", h=8),
                pattern=[[0, 8], [1, 128]], compare_op=ALU.is_ge, fill=0.0, base=0,
                channel_multiplier=-1), reads=[lB["wsT_f"]], writes=[lB["wsT_f"]])
            sc.op("pool", lambda E: E.tensor_copy(out=wsT[:], in_=wsT_f), reads=[lB["wsT_f"]], writes=[wsTB])
            for h in range(8):
                sc.op("dve", lambda E, h=h: E.tensor_scalar(
                    out=diag[:, h * 16:(h + 1) * 16], in0=ident16[:], scalar1=ws00[:, h:h + 1], scalar2=None,
                    op0=ALU.mult), reads=[identB, cB["ws00"]], writes=[diagB])
            selB = Buf("sel")
            sc.dma("sp", sel[:], d_sel, writes=[selB], sem="cin_sel")
            for (f_, A_, L_, dst_, key) in ((bs_f, bsA, bsL, bs2, "bs_f"), (bs0_f, bs0A, bs0L, bs02, "bs0_f")):
                sc.op("dve", lambda E, f_=f_, A_=A_: E.tensor_copy(out=A_, in_=f_),
                      reads=[lB[key]], writes=[bsB])
                sc.op("dve", lambda E, f_=f_, A_=A_, L_=L_: E.tensor_tensor(
                    out=L_, in0=f_, in1=A_, op=ALU.subtract), reads=[bsB, lB[key]], writes=[bsB])
                sc.op("dve", lambda E, A_=A_: E.tensor_scalar(
                    out=A_, in0=A_, scalar1=sel[:, 0:1], scalar2=None, op0=ALU.mult),
                    reads=[bsB, selB], writes=[bsB])
                sc.op("dve", lambda E, A_=A_, L_=L_, dst_=dst_: E.scalar_tensor_tensor(
                    out=dst_[0:2, :], in0=L_, scalar=sel[:, 1:2], in1=A_, op0=ALU.mult, op1=ALU.add),
                    reads=[bsB, selB], writes=[bsB])
            merge(allR1, list(lB.values()) + [bsB])

        warmB = Buf("actwarm")

        def act_warm_lnexp():
            sc.op("act", lambda E: E.activation(out=small[:, 40:41], in_=epsc, func=AF.Ln),
                  reads=[onesB], writes=[warmB])

        pending = {}

        def need_h(n):
            for k_ in sorted(pending):
                if k_ <= n + 3:
                    pending.pop(k_)()

        def flush_pending():
            for n in sorted(pending):
                pending.pop(n)()

        sq_of = {}

        def rms_a(n):
            c0, w = TT[n]
            idx = []
            for c in range(KC):
                i = nxt("sq", NSQ)
                idx.append(i)
                sc.op("act", lambda E, c=c, i=i: E.activation(
                    out=sq[i][:, 0:w], in_=xs(c, n), func=AF.Square),
                    reads=[xB[c][n]], writes=[sqB[i]])
            sq_of[n] = idx

        def rms_tile(n, gcol, final=False):
            if n not in sq_of:
                rms_a(n)
            rms_b(n, gcol, final)

        def rms_b(n, gcol, final=False):
            c0, w = TT[n]
            bk, bkB = nbank()
            idx = sq_of.pop(n)
            for c in range(KC):
                i = idx[c]
                sc.op("pe", lambda E, c=c, i=i: E.matmul(
                    bk[:, 0:w], lhsT=ones[:], rhs=sq[i][:, 0:w], start=(c == 0), stop=(c == KC - 1)),
                    reads=[sqB[i], onesB], writes=[bkB])
            sc.op("act", lambda E: E.activation(
                out=bk[:, 0:w], in_=bk[:, 0:w], func=AF.Ln, bias=epsc[:, 0:1], scale=1.0 / D),
                reads=[bkB, onesB], writes=[bkB])
            sc.op("act", lambda E: E.activation(
                out=bk[:, 0:w], in_=bk[:, 0:w], func=AF.Exp, scale=-0.5),
                reads=[bkB], writes=[bkB])
            for c in range(KC):
                if final:
                    sc.op("dve", lambda E, c=c: E.scalar_tensor_tensor(
                        out=xs(c, n), in0=xs(c, n), scalar=vecs[:, gcol + c:gcol + c + 1],
                        in1=bk[:, 0:w], op0=ALU.mult, op1=ALU.mult),
                        reads=[xB[c][n], bkB, cB["vecs"]], writes=[xB[c][n]])
                else:
                    sc.op("dve", lambda E, c=c: E.scalar_tensor_tensor(
                        out=hT[:, c, c0:c0 + w], in0=xs(c, n), scalar=vecs[:, gcol + c:gcol + c + 1],
                        in1=bk[:, 0:w], op0=ALU.mult, op1=ALU.mult),
                        reads=[xB[c][n], bkB, cB["vecs"]], writes=[hB[c][n]])

        def proj(bk, bkB, wt, wtB, wcol, n, src=None, srcB=None, nk=KC, wstride=None):
            c0, w = TT[n]
            if src is None:
                need_h(n)
            src_ = hT if src is None else src
            srcB_ = hB if srcB is None else srcB

            def f(E):
                last = None
                for k in range(nk):
                    o = k * wstride + wcol
                    last = E.matmul(bk[:, 0:w], lhsT=wt[:, o:o + 128], rhs=src_[:, k, c0:c0 + w],
                                    start=(k == 0), stop=(k == nk - 1))
                return last
            wtBs = wtB if isinstance(wtB, list) else [wtB]
            sc.op("pe", f, reads=wtBs + [srcB_[k][n] for k in range(nk)], writes=[bkB])

        def xacc(bk, bkB, m, n):
            c0, w = TT[n]
            sc.op("dve", lambda E: E.tensor_tensor(
                out=xs(m, n), in0=bk[:, 0:w], in1=xs(m, n), op=ALU.add),
                reads=[bkB, xB[m][n]], writes=[xB[m][n]])

        def out_proj(prefix):
            for h in range(2):
                wt, wtB = w_acquire(f"{prefix}_{h}")
                for n in range(5):
                    for mm in range(4):
                        m = h * 4 + mm
                        bk, bkB = nbank()
                        proj(bk, bkB, wt, wtB, mm * 128, n, src=mix, srcB=mB, wstride=512)
                        xacc(bk, bkB, m, n)
                    if h == 1:
                        yield n
                w_release()

        groups = ffn_groups()

        def ffn_up(l, g):
            chunks = groups[g]
            for jj in sorted(set(j // 2 for j in chunks)):
                wt, wtB = w_acquire(f"gu{l}_{jj}")
                for n in range(5):
                    c0, w = TT[n]
                    for ch in range(2):
                        j = 2 * jj + ch
                        slot = 4 * (g % 2) + (j - chunks[0])
                        bg_, bgB = nbank()
                        bu_, buB = nbank()
                        proj(bg_, bgB, wt, wtB, ch * 128, n, wstride=256)
                        proj(bu_, buB, wt, wtB, KC * 256 + ch * 128, n, wstride=256)
                        gi = nxt("gtmp", 2)
                        sc.op("act", lambda E, bg_=bg_, gi=gi, w=w: E.activation(
                            out=gtmp[gi][:, 0:w], in_=bg_[:, 0:w], func=AF.Silu),
                            reads=[bgB], writes=[gtmpB[gi]])
                        sc.op("dve", lambda E, bu_=bu_, gi=gi, w=w, c0=c0, slot=slot: E.tensor_tensor(
                            out=mix[:, slot, c0:c0 + w], in0=bu_[:, 0:w], in1=gtmp[gi][:, 0:w], op=ALU.mult),
                            reads=[buB, gtmpB[gi]], writes=[mB[slot][n]])
                w_release()

        def ffn_dn(l, g):
            chunks = groups[g]
            wt, wtB = w_acquire(f"dn{l}_{g}")
            nch = len(chunks)
            for n in range(5):
                c0, w = TT[n]
                for m in range(KC):
                    bk, bkB = nbank()

                    def f(E, bk=bk, m=m, c0=c0, w=w):
                        last = None
                        for jl in range(nch):
                            slot = 4 * (g % 2) + jl
                            last = E.matmul(bk[:, 0:w], lhsT=wt[:, jl * 1024 + m * 128: jl * 1024 + (m + 1) * 128],
                                            rhs=mix[:, slot, c0:c0 + w], start=(jl == 0), stop=(jl == nch - 1))
                        return last
                    sc.op("pe", f, reads=[wtB] + [mB[4 * (g % 2) + jl][n] for jl in range(nch)], writes=[bkB])
                    xacc(bk, bkB, m, n)
                yield n
            w_release()

        def ffn_dn2(l, ga, gb):
            wa, waB = w_acquire(f"dn{l}_{ga}")
            wb, wbB = w_acquire(f"dn{l}_{gb}")
            parts = [(wa, 4 * (ga % 2), len(groups[ga])), (wb, 4 * (gb % 2), len(groups[gb]))]
            tot = sum(p_[2] for p_ in parts)
            for n in range(5):
                c0, w = TT[n]
                plist = [parts] if n > 0 else [parts[0:1], parts[1:2]]
                for parts_ in plist:
                  tot = sum(p_[2] for p_ in parts_)
                  for m in range(KC):
                    bk, bkB = nbank()

                    def f(E, bk=bk, m=m, c0=c0, w=w, parts_=parts_, tot=tot):
                        last = None
                        i = 0
                        for (wt_, s0, nch_) in parts_:
                            for jl in range(nch_):
                                last = E.matmul(bk[:, 0:w],
                                                lhsT=wt_[:, jl * 1024 + m * 128: jl * 1024 + (m + 1) * 128],
                                                rhs=mix[:, s0 + jl, c0:c0 + w], start=(i == 0), stop=(i == tot - 1))
                                i += 1
                        return last
                    wr = [waB, wbB] if len(parts_) == 2 else ([waB] if parts_[0][0] is wa else [wbB])
                    rds = wr + [mB[s0 + jl][n] for (_, s0, nch_) in parts_ for jl in range(nch_)]
                    sc.op("pe", f, reads=rds, writes=[bkB])
                    xacc(bk, bkB, m, n)
                yield n
            w_release()
            w_release()

        v1_issued = [False]

        def ffn(l, after_tile, fuse_tail=False):
            G = len(groups)
            ffn_up(l, 0)
            for g in range(1, G):
                ffn_up(l, g)
                if g == G - 1:
                    act_warm_lnexp()
                if fuse_tail and g == G - 1:
                    break
                for _ in ffn_dn(l, g - 1):
                    pass
                if l == 0 and g - 1 == 2:
                    w_issue_special("v_1")
                    v1_issued[0] = True
            tail = ffn_dn2(l, G - 2, G - 1) if fuse_tail else ffn_dn(l, G - 1)
            done = []
            for n in tail:
                if done:
                    after_tile(done.pop(0))
                rms_a(n)
                done.append(n)
            pending[done[-1]] = (lambda k_=done[-1]: after_tile(k_))
            if l == 0 and not v1_issued[0]:
                w_issue_special("v_1")
                v1_issued[0] = True

        act_warm_lnexp()
        for n in range(5):
            pending[n] = (lambda n=n: rms_tile(n, VC_NMIX + 0))

        wt, wtB = w_acquire("in0_pool")
        need_h(0)

        def pool_mm(g, n):
            c0, w = TT[n]
            bk, bkB = nbank()
            sc.op("pe", lambda E: E.matmul(
                bk[:, 0:w], lhsT=wpool[:, g * 128:(g + 1) * 128], rhs=dbuf[:, c0:c0 + w],
                start=True, stop=True), reads=[cB["wpool"], DB[n]], writes=[bkB])
            sc.op("act", lambda E: E.activation(
                out=mix[:, g, c0:c0 + w], in_=bk[:, 0:w], func=AF.Copy,
                scale=vecs[:, VC_PSCALE + g:VC_PSCALE + g + 1]),
                reads=[bkB, cB["vecs"]], writes=[mB[g][n]])

        for g, wdw in enumerate(WINS):
            P, PB = Pb[g % 2], PbB[g % 2]
            for n in range(5):
                c0, w = TT[n]
                bk, bkB = nbank()
                proj(bk, bkB, wt, wtB, g * 128, n, wstride=512)
                sc.op("act", lambda E, bk=bk, c0=c0, w=w, P=P: E.activation(
                    out=P[:, 16 + c0:16 + c0 + w], in_=bk[:, 0:w], func=AF.Copy),
                    reads=[bkB], writes=[PB[n]])
            if g == 0:
                early_setup()
                sc.op("dve", lambda E: E.memset(Pb[1][:, 0:16], 0.0), writes=[PbB[1][0]])
            if g > 0:
                for n in range(5):
                    pool_mm(g - 1, n)
            cur, curB = P, PB
            step = 1
            k = 0
            while step < wdw:
                dst, dstB = (bufB, BB) if k % 2 == 0 else (bufC, CB)
                HALF = 1024
                sc.op("pool", lambda E, cur=cur, dst=dst, step=step: E.tensor_tensor(
                    out=dst[:, 16:16 + HALF], in0=cur[:, 16:16 + HALF], in1=cur[:, 16 - step:16 - step + HALF],
                    op=ALU.add), reads=list(curB[0:2]), writes=list(dstB[0:2]))
                sc.op("dve", lambda E, cur=cur, dst=dst, step=step: E.tensor_tensor(
                    out=dst[:, 16 + HALF:16 + SEQ], in0=cur[:, 16 + HALF:16 + SEQ],
                    in1=cur[:, 16 + HALF - step:16 - step + SEQ],
                    op=ALU.add), reads=list(curB[1:5]), writes=list(dstB[2:5]))
                cur, curB = dst, dstB
                step *= 2
                k += 1
            S, SB_ = cur, curB
            sc.op("dve", lambda E, S=S, g=g, P=P: E.tensor_tensor(
                out=S[:, 16 + SEQ:16 + T], in0=P[:, 16 + SEQ:16 + T], in1=hsum[:, g * 16:(g + 1) * 16],
                op=ALU.add), reads=[PB[4], hsumB], writes=[SB_[4]])
            nfix = wdw - 1
            for n in range(5):
                c0, w = TT[n]
                sc.op("dve", lambda E, S=S, wdw=wdw, P=P, c0=c0, w=w: E.scalar_tensor_tensor(
                    out=dbuf[:, c0:c0 + w], in0=S[:, 16 + c0:16 + c0 + w], scalar=1.0 / wdw,
                    in1=P[:, 16 + c0:16 + c0 + w], op0=ALU.mult, op1=ALU.subtract),
                    reads=[SB_[n], PB[n]], writes=[DB[n]])
                if n == 0:
                    sc.op("dve", lambda E, S=S, g=g, nfix=nfix: E.tensor_tensor(
                        out=small[:, 0:nfix], in0=S[:, 16:16 + nfix], in1=invc[:, g * 16:g * 16 + nfix],
                        op=ALU.mult), reads=[SB_[0], cB["invc"]], writes=[padB])
                    sc.op("dve", lambda E, nfix=nfix, P=P: E.tensor_tensor(
                        out=dbuf[:, 0:nfix], in0=small[:, 0:nfix], in1=P[:, 16:16 + nfix], op=ALU.subtract),
                        reads=[padB, PB[0], DB[0]], writes=[DB[0]])
            sc.op("act", lambda E, g=g, P=P: E.activation(
                out=opp_sb[:, g * 15:(g + 1) * 15], in_=P[:, 16 + SEQ - 15:16 + SEQ], func=AF.Copy),
                reads=[PB[4]], writes=[opsB])
            sc.op("act", lambda E, g=g, P=P: E.activation(
                out=ops4[:, g, :, 14:15], in_=P[:, 16 + SEQ:16 + T].rearrange("p (s o) -> p s o", o=1),
                func=AF.Copy), reads=[PB[4]], writes=[opsB])
        w_release()
        ev_pp = sc.dma("sp", o_pp, opp_sb[:], reads=[opsB], sem="out")
        ev_ps = sc.dma("sp", o_ps, ops_sb, reads=[opsB], sem="out")
        merge([mB[7][n] for n in range(5)], [opsB])
        merge(cXB + cCB, PbB[0] + BB + CB)
        merge([mB[slot][n] for slot in range(4, 7) for n in range(5)], PbB[1])

        for j in range(4):
            wt, wtB = w_acquire(f"in0_conv{j}")
            if j == 0:
                sc.op("dve", lambda E: E.memset(cbuf[:, 0:2], 0.0), writes=[cCB[0]])
            w0 = vecs[:, VC_CONVW + 0 * 4 + j:VC_CONVW + 0 * 4 + j + 1]
            w1 = vecs[:, VC_CONVW + 1 * 4 + j:VC_CONVW + 1 * 4 + j + 1]
            w2 = vecs[:, VC_CONVW + 2 * 4 + j:VC_CONVW + 2 * 4 + j + 1]
            for n in range(5):
                c0, w = TT[n]
                bx, bxB = nbank()
                bc, bcB = nbank()
                bb, bbB = nbank()
                proj(bx, bxB, wt, wtB, 0, n, wstride=384)
                proj(bc, bcB, wt, wtB, 128, n, wstride=384)
                proj(bb, bbB, wt, wtB, 256, n, wstride=384)
                if j == 0 and n == 1:
                    for nn in range(5):
                        pool_mm(3, nn)
                sc.op("act", lambda E, bx=bx, c0=c0, w=w: E.activation(
                    out=xsb[:, c0:c0 + w], in_=bx[:, 0:w], func=AF.Copy),
                    reads=[bxB], writes=[cXB[n]])
                sc.op("dve", lambda E, bc=bc, c0=c0, w=w: E.tensor_tensor(
                    out=cbuf[:, 2 + c0:2 + c0 + w], in0=bc[:, 0:w], in1=xsb[:, c0:c0 + w], op=ALU.mult),
                    reads=[bcB, cXB[n]], writes=[cCB[n]])
                yi = nxt("y", 2)
                yb = ybufs[yi]
                segs = [("p", c0, w)] if n < 4 else [("p", LASTP[0], LASTP[1]), ("s", SEQ, NS)]
                for kind, a0, aw in segs:
                    yo = a0 - c0
                    if kind == "p":
                        cm2 = cbuf[:, a0:a0 + aw]
                        cm1 = cbuf[:, 1 + a0:1 + a0 + aw]
                        crd = [cCB[n]] + ([cCB[n - 1]] if n > 0 else [])
                    else:
                        cm2 = hc4[:, j, :, 0:1].rearrange("p s o -> p (s o)")
                        cm1 = hc4[:, j, :, 1:2].rearrange("p s o -> p (s o)")
                        crd = [cB["hc"]]
                    sc.op("act", lambda E, cm2=cm2, aw=aw, yo=yo, w0=w0, yb=yb: E.activation(
                        out=yb[:, yo:yo + aw], in_=cm2, func=AF.Copy, scale=w0),
                        reads=crd + [cB["vecs"]], writes=[yB[yi]])
                    sc.op("dve", lambda E, cm1=cm1, aw=aw, yo=yo, w1=w1, yb=yb: E.scalar_tensor_tensor(
                        out=yb[:, yo:yo + aw], in0=cm1, scalar=w1, in1=yb[:, yo:yo + aw],
                        op0=ALU.mult, op1=ALU.add), reads=crd + [yB[yi]], writes=[yB[yi]])
                    sc.op("dve", lambda E, a0=a0, aw=aw, yo=yo, w2=w2, yb=yb: E.scalar_tensor_tensor(
                        out=yb[:, yo:yo + aw], in0=cbuf[:, 2 + a0:2 + a0 + aw], scalar=w2, in1=yb[:, yo:yo + aw],
                        op0=ALU.mult, op1=ALU.add), reads=[cCB[n], yB[yi]], writes=[yB[yi]])
                sc.op("dve", lambda E, bb=bb, c0=c0, w=w, j=j, yb=yb: E.tensor_tensor(
                    out=mix[:, 4 + j, c0:c0 + w], in0=bb[:, 0:w], in1=yb[:, 0:w], op=ALU.mult),
                    reads=[bbB, yB[yi]], writes=[mB[4 + j][n]])
                if n == 4:
                    sc.op("act", lambda E, j=j: E.activation(
                        out=ocp_sb[:, j * 2:(j + 1) * 2], in_=cbuf[:, 2 + SEQ - 2:2 + SEQ], func=AF.Copy),
                        reads=[cCB[4]], writes=[ocsB])
                if n == 4:
                    sc.op("act", lambda E, j=j: E.activation(
                        out=ocs4[:, j, :, 1:2], in_=cbuf[:, 2 + SEQ:2 + T].rearrange("p (s o) -> p s o", o=1),
                        func=AF.Copy), reads=[cCB[4]], writes=[ocsB])
            w_release()
        ev_cp = sc.dma("sp", o_cp, ocp_sb[:], reads=[ocsB], sem="out")
        ev_cs = sc.dma("sp", o_cs, ocs_sb[:], reads=[ocsB], sem="out")

        def run_outproj(prefix, gcol):
            done = []
            for n in out_proj(prefix):
                if done:
                    rms_b(done.pop(0), gcol)
                rms_a(n)
                done.append(n)
            pending[done[-1]] = (lambda k_=done[-1]: rms_b(k_, gcol))

        run_outproj("out0", VC_NFFN + 0)
        late_setup()
        ffn(0, lambda n: rms_tile(n, VC_NMIX + 8), fuse_tail=FUSE_L0)

        wv0, wv0B = w_acquire("v_0")
        wv1, wv1B = w_acquire("v_1")
        w_issue_special("u_0")

        def slotB(s_):
            return [mB[s_][n] for n in range(5)]
        ssB = Buf("ss")
        nhB = Buf("neghalf")
        sc.op("pool", lambda E: E.memset(small[:, 32:33], -0.5), writes=[nhB])
        sc.dma("sp", gv, d_gv, writes=slotB(1), sem="gvin")
        vN3 = vN.rearrange("p (i d) -> p i d", i=17)

        def tile_of(col):
            for n_, (c0_, w_) in enumerate(TT):
                if c0_ <= col < c0_ + w_:
                    return n_
            raise ValueError(col)
        for i in range(17):
            ntok = 128 if i < 16 else 16
            t0 = i * 128
            n = tile_of(t0)
            need_h(n)
            vt = vtmp2[i % 2]
            vtBs = slotB(6 + i % 2)
            for hh, (wv, wvB) in enumerate(((wv0, wv0B), (wv1, wv1B))):
                bk, bkB = nbank()

                def f(E, bk=bk, wv=wv, t0=t0, ntok=ntok):
                    last = None
                    for k in range(KC):
                        last = E.matmul(bk[0:ntok, :], lhsT=hT[:, k, t0:t0 + ntok], rhs=wv[:, k * 512:(k + 1) * 512],
                                        start=(k == 0), stop=(k == KC - 1))
                    return last
                wvBs = wvB if isinstance(wvB, list) else [wvB]
                sc.op("pe", f, reads=wvBs + [hB[k][n] for k in range(KC)], writes=[bkB])
                sc.op("act", lambda E, bk=bk, vt=vt, hh=hh, ntok=ntok: E.activation(
                    out=vt[0:ntok, hh * 512:(hh + 1) * 512], in_=bk[0:ntok, :], func=AF.Gelu),
                    reads=[bkB], writes=vtBs)
            sc.op("act", lambda E, vt=vt, i=i, ntok=ntok: E.activation(
                out=junk[0:ntok, :], in_=vt[0:ntok, :], func=AF.Square, accum_out=ssv[0:ntok, i:i + 1]),
                reads=vtBs, writes=slotB(0) + [ssB])
            sc.op("dve", lambda E, i=i, ntok=ntok: E.tensor_scalar(
                out=ssv[0:ntok, i:i + 1], in0=ssv[0:ntok, i:i + 1], scalar1=1.0 / D, scalar2=EPS,
                op0=ALU.mult, op1=ALU.add), reads=[ssB], writes=[ssB])
            sc.op("pool", lambda E, i=i, ntok=ntok: E.tensor_tensor(
                out=rsv[0:ntok, i:i + 1], in0=ssv[0:ntok, i:i + 1], in1=small[0:ntok, 32:33], op=ALU.pow),
                reads=[ssB, nhB], writes=[ssB])
            sc.op("dve", lambda E, vt=vt, i=i, ntok=ntok: E.scalar_tensor_tensor(
                out=vN3[0:ntok, i, :], in0=vt[0:ntok, :], scalar=rsv[0:ntok, i:i + 1], in1=gv[0:ntok, :],
                op0=ALU.mult, op1=ALU.mult), reads=vtBs + [ssB] + slotB(1),
                writes=[vNB[i]] + allR1)
            if i == 16:
                sc.op("dve", lambda E, vt=vt, i=i, ntok=ntok: E.scalar_tensor_tensor(
                    out=vout, in0=vt[0:ntok, :], scalar=rsv[0:ntok, i:i + 1], in1=gv[0:ntok, :],
                    op0=ALU.mult, op1=ALU.mult), reads=vtBs + [ssB] + slotB(1), writes=list(sqB[0:4]))
                ev_v = sc.dma("sp", o_v, vout, reads=list(sqB[0:4]), sem="out")
        w_release()

        for h in range(8):
            if h % 4 == 0:
                wt, wtB = w_acquire(f"u_{h // 4}")
                if DEBUG and h == 0:
                    sc.dma("pool", o_dbg, wt[:, 0:4096], reads=wtB, sem="dbg")
            for n in range(5):
                c0, w = TT[n]
                bu_, buB = nbank()
                proj(bu_, buB, wt, wtB, (h % 4) * 128, n, wstride=512)
                gi = nxt("gtmp", 2)
                sc.op("act", lambda E, bu_=bu_, gi=gi, w=w: E.activation(
                    out=gtmp[gi][:, 0:w], in_=bu_[:, 0:w], func=AF.Gelu),
                    reads=[buB], writes=[gtmpB[gi]])
                bm, bmB = nbank()
                pw = min(c0 + w, SEQ) - c0
                nq = pw // 128
                q0 = c0 // 128
                has_s = (c0 + w > SEQ)

                def f(E, bm=bm, h=h, pw=pw, nq=nq, q0=q0, has_s=has_s):
                    E.matmul(bm[:, 0:pw], lhsT=ones[:, :],
                             rhs=bs2[:, h * 128:(h + 1) * 128].unsqueeze(1).broadcast_to([128, nq, 128]),
                             start=True, stop=False)
                    if has_s:
                        E.matmul(bm[:, pw:pw + NS], lhsT=ones[:, :], rhs=bs02[:, h * 16:(h + 1) * 16],
                                 start=False, stop=False)
                    last = None
                    for q in range(nq):
                        last = E.matmul(bm[:, q * 128:(q + 1) * 128],
                                        lhsT=vN3[:, q0 + q, h * 128:(h + 1) * 128],
                                        rhs=wsT[:, h * 128:(h + 1) * 128], start=False,
                                        stop=(q == nq - 1 and not has_s))
                    if has_s:
                        last = E.matmul(bm[:, pw:pw + NS], lhsT=vN3[0:16, 16, h * 128:(h + 1) * 128],
                                        rhs=diag[0:16, h * 16:(h + 1) * 16], start=False, stop=True)
                    return last
                rd = [onesB, bsB, wsTB] + [vNB[q0 + q] for q in range(nq)]
                if has_s:
                    rd += [diagB, vNB[16]]
                sc.op("pe", f, reads=rd, writes=[bmB])
                sc.op("dve", lambda E, bm=bm, gi=gi, c0=c0, w=w, h=h: E.tensor_tensor(
                    out=mix[:, h, c0:c0 + w], in0=bm[:, 0:w], in1=gtmp[gi][:, 0:w], op=ALU.mult),
                    reads=[bmB, gtmpB[gi]], writes=[mB[h][n]])
            if h % 4 == 3 and f"u_{h // 4}" not in SPECIAL:
                w_release()

        act_warm_lnexp()
        run_outproj("out1", VC_NFFN + 8)

        out_evs = [ev_pp, ev_ps, ev_cp, ev_cs, ev_v]

        def final_tile(n):
            c0, w = TT[n]
            rms_tile(n, VC_NFIN, final=True)
            if n == 4:
                for hh, q_ in ((0, "sp"), (1, "act")):
                    out_evs.append(sc.dma(q_, o_y[:, xrange_(n, 4 * hh, 4 * hh + 4)], xT[:, xrange_(n, 4 * hh, 4 * hh + 4)],
                                          reads=[xB[c][n] for c in range(4 * hh, 4 * hh + 4)], sem=("out" if hh == 0 else "out2")))
                return
            out_evs.append(sc.dma("sp", o_y[:, xrange_(n)], xT[:, xrange_(n)],
                                  reads=[xB[c][n] for c in range(KC)], sem="out"))
        ffn(1, final_tile, fuse_tail=True)
        flush_pending()
        sc.wait("sp", [("out", sc.cnt["out"]), ("out2", sc.cnt["out2"])])
        sc.emit_all()
    return nc


_NC_CACHE = {}


def _get_program():
    if "nc" not in _NC_CACHE:
        _NC_CACHE["nc"] = build_program()
    return _NC_CACHE["nc"]


def _feat_major(a):
    n = a.shape[0]
    return a.T.reshape(KC, 128, n).transpose(1, 0, 2)


def kernel(x_prompt, x_sample, state_pool, state_conv, norm_mix, norm_ffn, norm_final,
           w_in_even, w_pool, pool_scale, conv_w, w_out_even,
           w_in_odd, norm_sg, w_s, b_s, w_out_odd,
           ffn_w_gate, ffn_w_up, ffn_w_down):
    inp = dict(x_prompt=x_prompt, x_sample=x_sample, state_pool=state_pool, state_conv=state_conv,
               norm_mix=norm_mix, norm_ffn=norm_ffn, norm_final=norm_final, w_in_even=w_in_even,
               w_pool=w_pool, pool_scale=pool_scale, conv_w=conv_w, w_out_even=w_out_even,
               w_in_odd=w_in_odd, norm_sg=norm_sg, w_s=w_s, b_s=b_s, w_out_odd=w_out_odd,
               ffn_w_gate=ffn_w_gate, ffn_w_up=ffn_w_up, ffn_w_down=ffn_w_down)
    inp = {k: np.asarray(v, dtype=np.float32) for k, v in inp.items()}
    nc = _get_program()

    wts = pack_weights(inp)
    vecs = pack_vecs(inp)
    gv_rep = np.ascontiguousarray(np.broadcast_to(inp["norm_sg"][0][None, :], (128, D)))
    wpool = np.ascontiguousarray(inp["w_pool"][0].transpose(1, 0, 2)).reshape(128, 512)
    wsT = np.ascontiguousarray(inp["w_s"][0].transpose(2, 0, 1)).reshape(128, 1024)
    ws00 = np.ascontiguousarray(np.broadcast_to(inp["w_s"][0][:, 0, 0][None, :], (16, 8)))
    bs = np.ascontiguousarray(np.broadcast_to(inp["b_s"][0].reshape(1, 1024), (2, 1024)))
    bs0 = np.ascontiguousarray(np.broadcast_to(np.repeat(inp["b_s"][0][:, 0], 16).reshape(1, 128), (2, 128)))
    sel = np.eye(2, dtype=np.float32)
    invc = np.empty((128, 4, 16), np.float32)
    for g, w in enumerate(WINS):
        invc[:, g, :] = 1.0 / np.minimum(np.arange(16) + 1, w).astype(np.float32)
    invc = invc.reshape(128, 64)

    in_maps = []
    for c in range(NCORES):
        xs = inp["x_sample"][c * NS:(c + 1) * NS, 0, :]
        xfm = np.concatenate([_feat_major(inp["x_prompt"][c]), _feat_major(xs)], axis=2)
        xT = np.ascontiguousarray(np.concatenate(
            [xfm[:, :, c0_:c0_ + w_].reshape(128, KC * w_) for (c0_, w_) in TT], axis=1))
        hp = np.ascontiguousarray(inp["state_pool"][0, c * NS:(c + 1) * NS].reshape(NS, 15, 4, 128)
                                  .transpose(3, 2, 0, 1)).reshape(128, 960)
        hc = np.ascontiguousarray(inp["state_conv"][0, c * NS:(c + 1) * NS].reshape(NS, 2, 4, 128)
                                  .transpose(3, 2, 0, 1)).reshape(128, 128)
        in_maps.append({"xT": xT, "wts": wts, "vecs": vecs, "gv_rep": gv_rep, "wpool": wpool,
                        "wsT": wsT, "ws00": ws00, "bs": bs, "bs0": bs0, "sel": sel, "invc": invc,
                        "hp": hp, "hc": hc})
    res = run_bass_kernel_spmd(nc, in_maps, core_ids=list(range(NCORES)))
    R = res.results

    y_prompt = np.empty((NCORES, SEQ, D), np.float32)
    y_sample = np.empty((NCORES * NS, 1, D), np.float32)
    pool_p = np.empty((1, NCORES, 15, 512), np.float32)
    pool_s = np.empty((1, NCORES * NS, 15, 512), np.float32)
    conv_p = np.empty((1, NCORES, 2, 512), np.float32)
    conv_s = np.empty((1, NCORES * NS, 2, 512), np.float32)
    v_s = np.empty((1, NCORES * NS, 1, D), np.float32)
    for c in range(NCORES):
        r = R[c]
        yflat = np.asarray(r["yT"]).reshape(128, KC * T)
        yT = np.empty((128, KC, T), np.float32)
        for n_, (c0_, w_) in enumerate(TT):
            yT[:, :, c0_:c0_ + w_] = yflat[:, XOFF[n_]:XOFF[n_] + KC * w_].reshape(128, KC, w_)
        y_prompt[c] = yT[:, :, :SEQ].transpose(2, 1, 0).reshape(SEQ, D)
        y_sample[c * NS:(c + 1) * NS, 0] = yT[:, :, SEQ:].transpose(2, 1, 0).reshape(NS, D)
        pool_p[0, c] = np.asarray(r["o_pp"]).reshape(128, 4, 15).transpose(2, 1, 0).reshape(15, 512)
        pool_s[0, c * NS:(c + 1) * NS] = (np.asarray(r["o_ps"]).reshape(128, 4, NS, 15)
                                          .transpose(2, 3, 1, 0).reshape(NS, 15, 512))
        conv_p[0, c] = np.asarray(r["o_cp"]).reshape(128, 4, 2).transpose(2, 1, 0).reshape(2, 512)
        conv_s[0, c * NS:(c + 1) * NS] = (np.asarray(r["o_cs"]).reshape(128, 4, NS, 2)
                                          .transpose(2, 3, 1, 0).reshape(NS, 2, 512))
        v_s[0, c * NS:(c + 1) * NS, 0] = np.asarray(r["o_v"]).reshape(NS, D)
    return (y_prompt, y_sample, pool_p, pool_s, conv_p, conv_s, v_s)
```

```python
import contextlib
import numpy as np
import concourse.bass as bass
import concourse.mybir as mybir
from concourse.bass_utils import run_bass_kernel_spmd

F32 = mybir.dt.float32
BF16 = mybir.dt.bfloat16
ALU = mybir.AluOpType
AF = mybir.ActivationFunctionType
AX = mybir.AxisListType

ENGS = ("pe", "act", "dve", "pool", "sp")

NCORES = 8
D = 1024
KC = 8
SEQ = 2048
NS = 16
T = SEQ + NS
TT = [(0, 512), (512, 512), (1024, 512), (1536, 256), (1792, 272)]
LASTP = (1792, 256)
XOFF = [KC * sum(w_ for _, w_ in TT[:n_]) for n_ in range(len(TT))]
DFF = 2816
NJ = DFF // 128
EPS = 1e-6
WINS = (2, 4, 8, 16)
WSLOT = 4096
NWSLOT = 3
FUSE_L0 = True
LAG = 1
DEBUG = False
SELF_SYNC = True


class Buf:
    __slots__ = ("w", "r", "name", "const")

    def __init__(self, name="", const=False):
        self.w = None
        self.r = {}
        self.name = name
        self.const = const


class Sched:
    def __init__(self, nc, stack, self_sync=True):
        self.nc = nc
        self.stack = stack
        self.ops = {e: [] for e in ENGS}
        self.sems = {}
        self.cnt = {}
        self.seen = {e: {} for e in ENGS}
        self.self_sync = self_sync
        for e in ENGS:
            self._sem(e)

    def _sem(self, key):
        if key not in self.sems:
            self.sems[key] = self.stack.enter_context(self.nc.semaphore("s_" + str(key)))
            self.cnt[key] = 0
        return self.sems[key]

    def _waits(self, eng, reads, writes, extra=()):
        deps = {}

        def add(ev):
            if ev is None:
                return
            k, v = ev
            if deps.get(k, 0) < v:
                deps[k] = v

        for b in reads:
            add(b.w)
        for b in writes:
            add(b.w)
            for k, v in b.r.items():
                add((k, v))
        for ev in extra:
            add(ev)
        waits = []
        seen = self.seen[eng]
        for k, v in deps.items():
            if k == eng and (eng == "pe" or not self.self_sync):
                continue
            if seen.get(k, 0) < v:
                seen[k] = v
                waits.append((k, v))
        return waits

    def _record(self, ev, reads, writes):
        for b in writes:
            b.w = ev
            b.r = {}
        k, v = ev
        for b in reads:
            if b.const:
                continue
            if b.r.get(k, 0) < v:
                b.r[k] = v

    def op(self, eng, fn, reads=(), writes=(), extra=()):
        waits = self._waits(eng, reads, writes, extra)
        self.cnt[eng] += 1
        ev = (eng, self.cnt[eng])
        sems = self.sems

        def emit(E, waits=waits, fn=fn, sem=sems[eng]):
            for k, v in waits:
                E.wait_ge(sems[k], v)
            ins = fn(E)
            ins.then_inc(sem, 1)

        self.ops[eng].append(emit)
        self._record(ev, reads, writes)
        return ev

    def dma(self, eng, out, in_, reads=(), writes=(), sem="dma", extra=(), **kw):
        waits = self._waits(eng, reads, writes, extra)
        s = self._sem(sem)
        self.cnt[sem] += 16
        ev = (sem, self.cnt[sem])
        sems = self.sems

        def emit(E, waits=waits, s=s):
            for k, v in waits:
                E.wait_ge(sems[k], v)
            E.dma_start(out=out, in_=in_, **kw).then_inc(s, 16)

        self.ops[eng].append(emit)
        self._record(ev, reads, writes)
        return ev

    def wait(self, eng, events):
        waits = self._waits(eng, (), (), extra=events)
        sems = self.sems

        def emit(E, waits=waits):
            for k, v in waits:
                E.wait_ge(sems[k], v)

        self.ops[eng].append(emit)

    def emit_all(self):
        nc = self.nc
        with nc.Block() as block:
            @block.tensor
            def _(E):
                for f in self.ops["pe"]:
                    f(E)

            @block.scalar
            def _(E):
                for f in self.ops["act"]:
                    f(E)

            @block.vector
            def _(E):
                for f in self.ops["dve"]:
                    f(E)

            @block.gpsimd
            def _(E):
                for f in self.ops["pool"]:
                    f(E)

            @block.sync
            def _(E):
                for f in self.ops["sp"]:
                    f(E)


def ffn_groups():
    sizes = (4, 4, 4, 4, 2, 4)
    gs = []
    j = 0
    for n in sizes:
        gs.append(list(range(j, j + n)))
        j += n
    assert j == NJ
    return gs


def weight_plan():
    plan = []
    plan.append(("in0_pool", KC * 512))
    for j in range(4):
        plan.append((f"in0_conv{j}", KC * 384))
    for h in range(2):
        plan.append((f"out0_{h}", KC * 512))

    def ffn(l):
        gs = ffn_groups()
        def up(g):
            for jj in sorted(set(j // 2 for j in gs[g])):
                plan.append((f"gu{l}_{jj}", 2 * KC * 256))

        def dn(g):
            plan.append((f"dn{l}_{g}", len(gs[g]) * 1024))
        G = len(gs)
        up(0)
        for g in range(1, G):
            up(g)
            dn(g - 1)
        dn(G - 1)
    ffn(0)
    for h in range(2):
        plan.append((f"v_{h}", KC * 512))
    for h in range(2):
        plan.append((f"u_{h}", KC * 512))
    for h in range(2):
        plan.append((f"out1_{h}", KC * 512))
    ffn(1)
    offs = {}
    o = 0
    for name, n in plan:
        offs[name] = (o, n)
        o += n
    return plan, offs, o


def _kmajor(w):
    C = w.shape[1]
    return w.reshape(KC, 128, C).transpose(1, 0, 2)


def pack_weights(inp):
    plan, offs, tot = weight_plan()
    out = np.empty((128, tot), np.float32)

    def put(name, arr):
        o, n = offs[name]
        a = np.ascontiguousarray(arr).reshape(128, -1)
        assert a.shape[1] == n, (name, a.shape, n)
        out[:, o:o + n] = a

    wi = inp["w_in_even"][0]
    put("in0_pool", _kmajor(wi[:, 0:512]))
    for j in range(4):
        cols = np.concatenate([np.arange(512 + j * 128, 512 + (j + 1) * 128),
                               np.arange(1536 + j * 128, 1536 + (j + 1) * 128),
                               np.arange(1024 + j * 128, 1024 + (j + 1) * 128)])
        put(f"in0_conv{j}", _kmajor(wi[:, cols]))
    wo = inp["w_out_even"][0]
    for h in range(2):
        put(f"out0_{h}", _kmajor(wo[:, h * 512:(h + 1) * 512]))
    gs = ffn_groups()
    for l in range(2):
        wg, wu, wd = inp["ffn_w_gate"][l], inp["ffn_w_up"][l], inp["ffn_w_down"][l]
        for jj in range(NJ // 2):
            g_ = _kmajor(wg[:, jj * 256:(jj + 1) * 256])
            u_ = _kmajor(wu[:, jj * 256:(jj + 1) * 256])
            put(f"gu{l}_{jj}", np.stack([g_, u_], axis=1))
        for g, chunks in enumerate(gs):
            r0 = chunks[0] * 128
            nch = len(chunks)
            blk = wd[r0:r0 + nch * 128, :].reshape(nch, 128, D).transpose(1, 0, 2)
            put(f"dn{l}_{g}", blk)
    wio = inp["w_in_odd"][0]
    for h in range(2):
        put(f"v_{h}", _kmajor(wio[:, 1024 + h * 512:1024 + (h + 1) * 512]))
    for h in range(2):
        put(f"u_{h}", _kmajor(wio[:, h * 512:(h + 1) * 512]))
    woo = inp["w_out_odd"][0]
    for h in range(2):
        put(f"out1_{h}", _kmajor(woo[:, h * 512:(h + 1) * 512]))
    return out


VC_NMIX = 0
VC_NFFN = 16
VC_NFIN = 32
VC_PSCALE = 40
VC_CONVW = 44
NVEC = 56


def pack_vecs(inp):
    v = np.empty((128, NVEC), np.float32)

    def cols(a):
        return a.reshape(-1, 128).T
    v[:, VC_NMIX:VC_NMIX + 16] = cols(inp["norm_mix"].reshape(-1))
    v[:, VC_NFFN:VC_NFFN + 16] = cols(inp["norm_ffn"].reshape(-1))
    v[:, VC_NFIN:VC_NFIN + 8] = cols(inp["norm_final"])
    v[:, VC_PSCALE:VC_PSCALE + 4] = cols(inp["pool_scale"][0])
    v[:, VC_CONVW:VC_CONVW + 12] = cols(inp["conv_w"][0].reshape(-1))
    return v


def build_program():
    plan, offs, WTOT = weight_plan()
    nc = bass.Bass("TRN2", target_bir_lowering=False)

    def din(name, shape):
        return nc.dram_tensor(name, list(shape), F32, kind="ExternalInput").ap()

    def dout(name, shape):
        return nc.dram_tensor(name, list(shape), F32, kind="ExternalOutput").ap()

    d_x = din("xT", (128, KC * T))
    d_w = din("wts", (128, WTOT))
    d_vecs = din("vecs", (128, NVEC))
    d_gv = din("gv_rep", (128, D))
    d_wpool = din("wpool", (128, 4 * 128))
    d_wsT = din("wsT", (128, 8 * 128))
    d_ws00 = din("ws00", (16, 8))
    d_bs = din("bs", (2, 8 * 128))
    d_bs0 = din("bs0", (2, 8 * 16))
    d_sel = din("sel", (2, 2))
    d_invc = din("invc", (128, 4 * 16))
    d_hp = din("hp", (128, 4 * 16 * 15))
    d_hc = din("hc", (128, 4 * 16 * 2))

    o_y = dout("yT", (128, KC * T))
    o_pp = dout("o_pp", (128, 4 * 15))
    o_ps = dout("o_ps", (128, 4 * 16 * 15))
    o_cp = dout("o_cp", (128, 4 * 2))
    o_cs = dout("o_cs", (128, 4 * 16 * 2))
    o_v = dout("o_v", (16, D))
    if DEBUG:
        o_dbg = dout("dbg", (128, 4096))

    with contextlib.ExitStack() as st:
        sc = Sched(nc, st, self_sync=SELF_SYNC)

        def sb(name, shape, dt):
            return st.enter_context(nc.sbuf_tensor(name, list(shape), dt))

        xT = sb("xTs", (128, KC * T), F32)

        def xs(c, n):
            w_ = TT[n][1]
            return xT[:, XOFF[n] + c * w_:XOFF[n] + (c + 1) * w_]

        def xrange_(n, c_lo=0, c_hi=KC):
            w_ = TT[n][1]
            return slice(XOFF[n] + c_lo * w_, XOFF[n] + c_hi * w_)
        hT = sb("hTs", (128, KC, T), BF16)
        mix = sb("mix", (128, 8, T), BF16)
        R1 = sb("R1", (128, 8704), F32)
        wring = [sb(f"wring{i}", (128, WSLOT), BF16) for i in range(NWSLOT)]
        NSQ = 8
        sqall = sb("sqall", (128, NSQ * 512), BF16)
        sq = [sqall[:, i * 512:(i + 1) * 512] for i in range(NSQ)]
        gtmp = [sb(f"gtmp{i}", (128, 512), F32) for i in range(2)]
        vecs = sb("vecs_s", (128, NVEC), F32)
        wpool = sb("wpool_s", (128, 4 * 128), BF16)
        wsT = sb("wsT_s", (128, 8 * 128), BF16)
        ws00 = sb("ws00_s", (16, 8), F32)
        ident16 = sb("ident16", (16, 16), F32)
        diag = sb("diag", (16, 8 * 16), BF16)
        bs2 = sb("bs2", (128, 8 * 128), BF16)
        bs02 = sb("bs02", (128, 8 * 16), BF16)
        sel = sb("sel_s", (2, 2), F32)
        invc = sb("invc_s", (128, 4 * 16), F32)
        opp_sb = sb("opp_sb", (128, 4 * 15), F32)
        hc = sb("hc_s", (128, 4 * 16 * 2), F32)
        ocs_sb = sb("ocs_sb", (128, 4 * 16 * 2), F32)
        ocp_sb = sb("ocp_sb", (128, 4 * 2), F32)
        hsum = sb("hsum", (128, 4 * 16), F32)
        ones = sb("ones", (128, 128), BF16)
        ssv = sb("ssv", (128, 17), F32)
        rsv = sb("rsv", (128, 17), F32)
        small = sb("small", (128, 64), F32)
        mixflat = mix[:].rearrange("p s t -> p (s t)")
        scr = mixflat[:, 4 * T:8 * T].bitcast(F32)
        hp = scr[:, 2048:3008]
        ops_sb = mixflat[:, 7 * T:7 * T + 1920].bitcast(F32)
        vtmp2 = [mixflat[:, (6 + i) * T:(6 + i) * T + 2048].bitcast(F32) for i in range(2)]
        junk = mixflat[:, 0:1024]
        gv = mixflat[:, T:T + 2048].bitcast(F32)
        vout = sqall[0:16, 0:2048].bitcast(F32)
        banks = [st.enter_context(nc.psum_tensor(f"bank{i}", [128, 512], F32)) for i in range(8)]
        print("sbuf bytes remaining:", nc.sbuf_bytes_remaining)

        PW = 16 + T
        Pb = [R1[:, 0:PW], mixflat[:, 4 * T:4 * T + 2 * PW].bitcast(F32)]
        bufB = R1[:, PW:2 * PW]
        bufC = R1[:, 2 * PW:3 * PW]
        dbuf = R1[:, 3 * PW:3 * PW + T // 2].bitcast(BF16)
        YO = 3 * PW + T // 2
        ybufs = [R1[:, YO:YO + 512], R1[:, YO + 512:YO + 1024]]
        CW = 2 + T
        xsb = R1[:, 0:T]
        cbuf = R1[:, PW:PW + CW]
        vN = R1[:, 0:17 * 512].bitcast(BF16)

        xB = [[Buf(f"x{c}_{n}") for n in range(5)] for c in range(KC)]
        hB = [[Buf(f"h{c}_{n}") for n in range(5)] for c in range(KC)]
        mB = [[Buf(f"m{c}_{n}") for n in range(5)] for c in range(8)]
        bankB = [Buf(f"bank{i}") for i in range(8)]
        wB = [Buf(f"w{i}") for i in range(NWSLOT)]
        sqB = [Buf(f"sq{i}") for i in range(NSQ)]
        gtmpB = [Buf(f"gtmp{i}") for i in range(2)]
        cst = Buf("const")
        PbB = [[Buf(f"P{i}_{n}") for n in range(5)] for i in range(2)]
        BB = [Buf(f"B_{n}") for n in range(5)]
        CB = [Buf(f"C_{n}") for n in range(5)]
        DB = [Buf(f"D_{n}") for n in range(5)]
        cXB = [Buf(f"cX_{n}") for n in range(5)]
        cCB = [Buf(f"cC_{n}") for n in range(5)]
        yB = [Buf("y0"), Buf("y1")]
        allR1 = PbB[0] + BB + CB + DB + cXB + cCB + yB

        def merge(dsts, srcs):
            for s_ in srcs:
                evs = ([s_.w] if s_.w is not None else []) + list(s_.r.items())
                for d_ in dsts:
                    for (k_, v_) in evs:
                        if d_.r.get(k_, 0) < v_:
                            d_.r[k_] = v_
        vNB = [Buf(f"vN{i}") for i in range(17)]

        st_ = {"bank": 0, "sq": 0, "rstd": 0, "gtmp": 0, "y": 0}

        def nxt(kind, n):
            i = st_[kind]
            st_[kind] = (i + 1) % n
            return i

        def nbank():
            i = nxt("bank", 8)
            return banks[i], bankB[i]

        wstate = {"issued": 0, "acq": 0, "ring_i": 0, "ring_a": 0}
        SPECIAL = ("u_0", "v_1")
        wX = mixflat[:, 4 * T:4 * T + WSLOT]

        def wXB():
            return [mB[4][n] for n in range(5)] + [mB[5][n] for n in range(5)]

        def w_issue():
            i = wstate["issued"]
            while i < len(plan) and plan[i][0] in SPECIAL:
                i += 1
            if i >= len(plan):
                wstate["issued"] = i
                return
            name, n = plan[i]
            o, _ = offs[name]
            slot = wstate["ring_i"] % NWSLOT
            wstate["ring_i"] += 1
            sc.dma("pool", wring[slot][:, 0:n], d_w[:, o:o + n], writes=[wB[slot]], sem=f"w{slot}")
            wstate["issued"] = i + 1

        wY = mixflat[:, 2 * T:2 * T + WSLOT]

        def spec_slot(name):
            if name == "u_0":
                return wX, wXB(), "wX"
            return wY, [mB[2][n] for n in range(5)] + [mB[3][n] for n in range(5)], "wY"

        def w_issue_special(name):
            o, n = offs[name]
            dst, bufs, sem_ = spec_slot(name)
            sc.dma("pool", dst[:, 0:n], d_w[:, o:o + n], writes=bufs, sem=sem_)

        def w_acquire(name):
            i = wstate["acq"]
            assert plan[i][0] == name, (plan[i][0], name)
            wstate["acq"] = i + 1
            if name in SPECIAL:
                dst, bufs, _ = spec_slot(name)
                return dst, bufs
            slot = wstate["ring_a"] % NWSLOT
            wstate["ring_a"] += 1
            return wring[slot], wB[slot]

        def w_release():
            w_issue()

        cB = {}

        def cload(name, dst, src, eng="sp"):
            cB[name] = Buf(name)
            sc.dma(eng, dst, src, writes=[cB[name]], sem=("cin" if eng == "sp" else "cinp"))

        xev = []
        for n, (c0, w) in enumerate(TT):
            if n == 0:
                for hh in range(2):
                    sc.dma("sp", xT[:, xrange_(n, 4 * hh, 4 * hh + 4)], d_x[:, xrange_(n, 4 * hh, 4 * hh + 4)],
                           writes=[xB[c][n] for c in range(4 * hh, 4 * hh + 4)], sem=f"xin0_{hh}")
                xev.append(xB[KC - 1][0].w)
                cload("vecs", vecs[:], d_vecs)
                continue
            xev.append(sc.dma("act" if n % 2 == 1 else "sp", xT[:, xrange_(n)], d_x[:, xrange_(n)],
                              writes=[xB[c][n] for c in range(KC)], sem=f"xin{n}"))
            if n == 1:
                cload("invc", invc[:], d_invc)
                cload("hp", hp, d_hp)
                cload("hc", hc[:], d_hc)
                cload("ws00", ws00[:], d_ws00)
        cin_total = ("cin", sc.cnt["cin"])
        for k_, b in cB.items():
            b.w = cin_total

        onesB = Buf("ones")
        sc.op("pool", lambda E: E.memset(ones[:], 1.0), writes=[onesB])
        epsc = small[:, 33:34]
        bsB = Buf("bs")
        sc.op("pool", lambda E: E.memset(bs2[:], 0.0), writes=[bsB])
        sc.op("pool", lambda E: E.memset(bs02[:], 0.0), writes=[bsB])
        sc.op("pool", lambda E: E.memset(epsc, EPS), writes=[onesB])
        padB = Buf("pad")
        sc.op("pool", lambda E: E.memset(R1[:, 0:3 * PW], 0.0), writes=PbB[0] + BB + CB)
        identB = Buf("ident")
        sc.op("pool", lambda E: E.memset(ident16[:], 0.0), writes=[identB])
        sc.op("pool", lambda E: E.affine_select(
            out=ident16[:], in_=ident16[:], pattern=[[-1, 16]], compare_op=ALU.not_equal,
            fill=1.0, base=0, channel_multiplier=1), reads=[identB], writes=[identB])
        cload("wpool", wpool[:], d_wpool, eng="pool")
        sc.wait("pool", [xev[0]])
        w_issue()
        sc.wait("pool", [xev[4]])
        w_issue()
        w_issue()
        hsumB = Buf("hsum")
        opsB = Buf("ops")
        ocsB = Buf("ocs")
        hp4 = hp.rearrange("p (g s r) -> p g s r", g=4, s=16)
        ops4 = ops_sb.rearrange("p (g s r) -> p g s r", g=4, s=16)
        hc4 = hc[:].rearrange("p (j s r) -> p j s r", j=4, s=16)
        ocs4 = ocs_sb[:].rearrange("p (j s r) -> p j s r", j=4, s=16)

        def early_setup():
            for g, wdw in enumerate(WINS):
                sc.op("dve", lambda E, g=g, wdw=wdw: E.tensor_reduce(
                    out=hsum[:, g * 16:(g + 1) * 16], in_=hp4[:, g, :, 15 - (wdw - 1):15],
                    axis=AX.X, op=ALU.add), reads=[cB["hp"]], writes=[hsumB])
            for g in range(4):
                sc.op("dve", lambda E, g=g: E.tensor_copy(out=ops4[:, g, :, 0:14], in_=hp4[:, g, :, 1:15]),
                      reads=[cB["hp"]], writes=[opsB])
            sc.op("dve", lambda E: E.tensor_copy(out=ocs4[:, :, :, 0:1], in_=hc4[:, :, :, 1:2]),
                  reads=[cB["hc"]], writes=[ocsB])
            merge([mB[slot][n] for slot in range(4, 8) for n in range(5)] + PbB[1], [cB["hp"]])

        wsTB = Buf("wsT")
        diagB = Buf("diag")
        wsT_f = R1[:, 0:1024]
        bs_f = R1[0:2, 1024:2048]
        bs0_f = R1[0:2, 2048:2176]
        bsA = R1[0:2, 2176:2176 + 512].bitcast(BF16)
        bsL = R1[0:2, 2688:2688 + 512].bitcast(BF16)
        bs0A = R1[0:2, 3200:3200 + 64].bitcast(BF16)
        bs0L = R1[0:2, 3264:3264 + 64].bitcast(BF16)

        def late_setup():
            lB = {}
            for name, dst, src in (("wsT_f", wsT_f, d_wsT), ("bs_f", bs_f, d_bs), ("bs0_f", bs0_f, d_bs0)):
                lB[name] = Buf(name)
                sc.dma("sp", dst, src, writes=[lB[name]] + allR1, sem="cin_" + name)
            sc.op("pool", lambda E: E.affine_select(
                out=wsT_f.rearrange("p (h t) -> p h t", h=8),
                in_=wsT_f.rearrange("p (h t) -> p h t", h=8),
                pattern=[[0, 8], [1, 128]], compare_op=ALU.is_ge, fill=0.0, base=0,
                channel_multiplier=-1), reads=[lB["wsT_f"]], writes=[lB["wsT_f"]])
            sc.op("pool", lambda E: E.tensor_copy(out=wsT[:], in_=wsT_f), reads=[lB["wsT_f"]], writes=[wsTB])
            for h in range(8):
                sc.op("dve", lambda E, h=h: E.tensor_scalar(
                    out=diag[:, h * 16:(h + 1) * 16], in0=ident16[:], scalar1=ws00[:, h:h + 1], scalar2=None,
                    op0=ALU.mult), reads=[identB, cB["ws00"]], writes=[diagB])
            selB = Buf("sel")
            sc.dma("sp", sel[:], d_sel, writes=[selB], sem="cin_sel")
            for (f_, A_, L_, dst_, key) in ((bs_f, bsA, bsL, bs2, "bs_f"), (bs0_f, bs0A, bs0L, bs02, "bs0_f")):
                sc.op("dve", lambda E, f_=f_, A_=A_: E.tensor_copy(out=A_, in_=f_),
                      reads=[lB[key]], writes=[bsB])
                sc.op("dve", lambda E, f_=f_, A_=A_, L_=L_: E.tensor_tensor(
                    out=L_, in0=f_, in1=A_, op=ALU.subtract), reads=[bsB, lB[key]], writes=[bsB])
                sc.op("dve", lambda E, A_=A_: E.tensor_scalar(
                    out=A_, in0=A_, scalar1=sel[:, 0:1], scalar2=None, op0=ALU.mult),
                    reads=[bsB, selB], writes=[bsB])
                sc.op("dve", lambda E, A_=A_, L_=L_, dst_=dst_: E.scalar_tensor_tensor(
                    out=dst_[0:2, :], in0=L_, scalar=sel[:, 1:2], in1=A_, op0=ALU.mult, op1=ALU.add),
                    reads=[bsB, selB], writes=[bsB])
            merge(allR1, list(lB.values()) + [bsB])

        warmB = Buf("actwarm")

        def act_warm_lnexp():
            sc.op("act", lambda E: E.activation(out=small[:, 40:41], in_=epsc, func=AF.Ln),
                  reads=[onesB], writes=[warmB])

        pending = {}

        def need_h(n):
            for k_ in sorted(pending):
                if k_ <= n + 3:
                    pending.pop(k_)()

        def flush_pending():
            for n in sorted(pending):
                pending.pop(n)()

        sq_of = {}

        def rms_a(n):
            c0, w = TT[n]
            idx = []
            for c in range(KC):
                i = nxt("sq", NSQ)
                idx.append(i)
                sc.op("act", lambda E, c=c, i=i: E.activation(
                    out=sq[i][:, 0:w], in_=xs(c, n), func=AF.Square),
                    reads=[xB[c][n]], writes=[sqB[i]])
            sq_of[n] = idx

        def rms_tile(n, gcol, final=False):
            if n not in sq_of:
                rms_a(n)
            rms_b(n, gcol, final)

        def rms_b(n, gcol, final=False):
            c0, w = TT[n]
            bk, bkB = nbank()
            idx = sq_of.pop(n)
            for c in range(KC):
                i = idx[c]
                sc.op("pe", lambda E, c=c, i=i: E.matmul(
                    bk[:, 0:w], lhsT=ones[:], rhs=sq[i][:, 0:w], start=(c == 0), stop=(c == KC - 1)),
                    reads=[sqB[i], onesB], writes=[bkB])
            sc.op("act", lambda E: E.activation(
                out=bk[:, 0:w], in_=bk[:, 0:w], func=AF.Ln, bias=epsc[:, 0:1], scale=1.0 / D),
                reads=[bkB, onesB], writes=[bkB])
            sc.op("act", lambda E: E.activation(
                out=bk[:, 0:w], in_=bk[:, 0:w], func=AF.Exp, scale=-0.5),
                reads=[bkB], writes=[bkB])
            for c in range(KC):
                if final:
                    sc.op("dve", lambda E, c=c: E.scalar_tensor_tensor(
                        out=xs(c, n), in0=xs(c, n), scalar=vecs[:, gcol + c:gcol + c + 1],
                        in1=bk[:, 0:w], op0=ALU.mult, op1=ALU.mult),
                        reads=[xB[c][n], bkB, cB["vecs"]], writes=[xB[c][n]])
                else:
                    sc.op("dve", lambda E, c=c: E.scalar_tensor_tensor(
                        out=hT[:, c, c0:c0 + w], in0=xs(c, n), scalar=vecs[:, gcol + c:gcol + c + 1],
                        in1=bk[:, 0:w], op0=ALU.mult, op1=ALU.mult),
                        reads=[xB[c][n], bkB, cB["vecs"]], writes=[hB[c][n]])

        def proj(bk, bkB, wt, wtB, wcol, n, src=None, srcB=None, nk=KC, wstride=None):
            c0, w = TT[n]
            if src is None:
                need_h(n)
            src_ = hT if src is None else src
            srcB_ = hB if srcB is None else srcB

            def f(E):
                last = None
                for k in range(nk):
                    o = k * wstride + wcol
                    last = E.matmul(bk[:, 0:w], lhsT=wt[:, o:o + 128], rhs=src_[:, k, c0:c0 + w],
                                    start=(k == 0), stop=(k == nk - 1))
                return last
            wtBs = wtB if isinstance(wtB, list) else [wtB]
            sc.op("pe", f, reads=wtBs + [srcB_[k][n] for k in range(nk)], writes=[bkB])

        def xacc(bk, bkB, m, n):
            c0, w = TT[n]
            sc.op("dve", lambda E: E.tensor_tensor(
                out=xs(m, n), in0=bk[:, 0:w], in1=xs(m, n), op=ALU.add),
                reads=[bkB, xB[m][n]], writes=[xB[m][n]])

        def out_proj(prefix):
            for h in range(2):
                wt, wtB = w_acquire(f"{prefix}_{h}")
                for n in range(5):
                    for mm in range(4):
                        m = h * 4 + mm
                        bk, bkB = nbank()
                        proj(bk, bkB, wt, wtB, mm * 128, n, src=mix, srcB=mB, wstride=512)
                        xacc(bk, bkB, m, n)
                    if h == 1:
                        yield n
                w_release()

        groups = ffn_groups()

        def ffn_up(l, g):
            chunks = groups[g]
            for jj in sorted(set(j // 2 for j in chunks)):
                wt, wtB = w_acquire(f"gu{l}_{jj}")
                for n in range(5):
                    c0, w = TT[n]
                    for ch in range(2):
                        j = 2 * jj + ch
                        slot = 4 * (g % 2) + (j - chunks[0])
                        bg_, bgB = nbank()
                        bu_, buB = nbank()
                        proj(bg_, bgB, wt, wtB, ch * 128, n, wstride=256)
                        proj(bu_, buB, wt, wtB, KC * 256 + ch * 128, n, wstride=256)
                        gi = nxt("gtmp", 2)
                        sc.op("act", lambda E, bg_=bg_, gi=gi, w=w: E.activation(
                            out=gtmp[gi][:, 0:w], in_=bg_[:, 0:w], func=AF.Silu),
                            reads=[bgB], writes=[gtmpB[gi]])
                        sc.op("dve", lambda E, bu_=bu_, gi=gi, w=w, c0=c0, slot=slot: E.tensor_tensor(
                            out=mix[:, slot, c0:c0 + w], in0=bu_[:, 0:w], in1=gtmp[gi][:, 0:w], op=ALU.mult),
                            reads=[buB, gtmpB[gi]], writes=[mB[slot][n]])
                w_release()

        def ffn_dn(l, g):
            chunks = groups[g]
            wt, wtB = w_acquire(f"dn{l}_{g}")
            nch = len(chunks)
            for n in range(5):
                c0, w = TT[n]
                for m in range(KC):
                    bk, bkB = nbank()

                    def f(E, bk=bk, m=m, c0=c0, w=w):
                        last = None
                        for jl in range(nch):
                            slot = 4 * (g % 2) + jl
                            last = E.matmul(bk[:, 0:w], lhsT=wt[:, jl * 1024 + m * 128: jl * 1024 + (m + 1) * 128],
                                            rhs=mix[:, slot, c0:c0 + w], start=(jl == 0), stop=(jl == nch - 1))
                        return last
                    sc.op("pe", f, reads=[wtB] + [mB[4 * (g % 2) + jl][n] for jl in range(nch)], writes=[bkB])
                    xacc(bk, bkB, m, n)
                yield n
            w_release()

        def ffn_dn2(l, ga, gb):
            wa, waB = w_acquire(f"dn{l}_{ga}")
            wb, wbB = w_acquire(f"dn{l}_{gb}")
            parts = [(wa, 4 * (ga % 2), len(groups[ga])), (wb, 4 * (gb % 2), len(groups[gb]))]
            tot = sum(p_[2] for p_ in parts)
            for n in range(5):
                c0, w = TT[n]
                plist = [parts] if n > 0 else [parts[0:1], parts[1:2]]
                for parts_ in plist:
                  tot = sum(p_[2] for p_ in parts_)
                  for m in range(KC):
                    bk, bkB = nbank()

                    def f(E, bk=bk, m=m, c0=c0, w=w, parts_=parts_, tot=tot):
                        last = None
                        i = 0
                        for (wt_, s0, nch_) in parts_:
                            for jl in range(nch_):
                                last = E.matmul(bk[:, 0:w],
                                                lhsT=wt_[:, jl * 1024 + m * 128: jl * 1024 + (m + 1) * 128],
                                                rhs=mix[:, s0 + jl, c0:c0 + w], start=(i == 0), stop=(i == tot - 1))
                                i += 1
                        return last
                    wr = [waB, wbB] if len(parts_) == 2 else ([waB] if parts_[0][0] is wa else [wbB])
                    rds = wr + [mB[s0 + jl][n] for (_, s0, nch_) in parts_ for jl in range(nch_)]
                    sc.op("pe", f, reads=rds, writes=[bkB])
                    xacc(bk, bkB, m, n)
                yield n
            w_release()
            w_release()

        v1_issued = [False]

        def ffn(l, after_tile, fuse_tail=False):
            G = len(groups)
            ffn_up(l, 0)
            for g in range(1, G):
                ffn_up(l, g)
                if g == G - 1:
                    act_warm_lnexp()
                if fuse_tail and g == G - 1:
                    break
                for _ in ffn_dn(l, g - 1):
                    pass
                if l == 0 and g - 1 == 2:
                    w_issue_special("v_1")
                    v1_issued[0] = True
            tail = ffn_dn2(l, G - 2, G - 1) if fuse_tail else ffn_dn(l, G - 1)
            done = []
            for n in tail:
                if done:
                    after_tile(done.pop(0))
                rms_a(n)
                done.append(n)
            pending[done[-1]] = (lambda k_=done[-1]: after_tile(k_))
            if l == 0 and not v1_issued[0]:
                w_issue_special("v_1")
                v1_issued[0] = True

        act_warm_lnexp()
        for n in range(5):
            pending[n] = (lambda n=n: rms_tile(n, VC_NMIX + 0))

        wt, wtB = w_acquire("in0_pool")
        flush_pending()

        def pool_mm(g, n):
            c0, w = TT[n]
            bk, bkB = nbank()
            sc.op("pe", lambda E: E.matmul(
                bk[:, 0:w], lhsT=wpool[:, g * 128:(g + 1) * 128], rhs=dbuf[:, c0:c0 + w],
                start=True, stop=True), reads=[cB["wpool"], DB[n]], writes=[bkB])
            sc.op("act", lambda E: E.activation(
                out=mix[:, g, c0:c0 + w], in_=bk[:, 0:w], func=AF.Copy,
                scale=vecs[:, VC_PSCALE + g:VC_PSCALE + g + 1]),
                reads=[bkB, cB["vecs"]], writes=[mB[g][n]])

        for g, wdw in enumerate(WINS):
            P, PB = Pb[g % 2], PbB[g % 2]
            for n in range(5):
                c0, w = TT[n]
                bk, bkB = nbank()
                proj(bk, bkB, wt, wtB, g * 128, n, wstride=512)
                sc.op("act", lambda E, bk=bk, c0=c0, w=w, P=P: E.activation(
                    out=P[:, 16 + c0:16 + c0 + w], in_=bk[:, 0:w], func=AF.Copy),
                    reads=[bkB], writes=[PB[n]])
            if g == 0:
                early_setup()
                sc.op("dve", lambda E: E.memset(Pb[1][:, 0:16], 0.0), writes=[PbB[1][0]])
            if g > 0:
                for n in range(5):
                    pool_mm(g - 1, n)
            cur, curB = P, PB
            step = 1
            k = 0
            while step < wdw:
                dst, dstB = (bufB, BB) if k % 2 == 0 else (bufC, CB)
                HALF = 1024
                sc.op("pool", lambda E, cur=cur, dst=dst, step=step: E.tensor_tensor(
                    out=dst[:, 16:16 + HALF], in0=cur[:, 16:16 + HALF], in1=cur[:, 16 - step:16 - step + HALF],
                    op=ALU.add), reads=list(curB[0:2]), writes=list(dstB[0:2]))
                sc.op("dve", lambda E, cur=cur, dst=dst, step=step: E.tensor_tensor(
                    out=dst[:, 16 + HALF:16 + SEQ], in0=cur[:, 16 + HALF:16 + SEQ],
                    in1=cur[:, 16 + HALF - step:16 - step + SEQ],
                    op=ALU.add), reads=list(curB[1:5]), writes=list(dstB[2:5]))
                cur, curB = dst, dstB
                step *= 2
                k += 1
            S, SB_ = cur, curB
            sc.op("dve", lambda E, S=S, g=g, P=P: E.tensor_tensor(
                out=S[:, 16 + SEQ:16 + T], in0=P[:, 16 + SEQ:16 + T], in1=hsum[:, g * 16:(g + 1) * 16],
                op=ALU.add), reads=[PB[4], hsumB], writes=[SB_[4]])
            nfix = wdw - 1
            for n in range(5):
                c0, w = TT[n]
                sc.op("dve", lambda E, S=S, wdw=wdw, P=P, c0=c0, w=w: E.scalar_tensor_tensor(
                    out=dbuf[:, c0:c0 + w], in0=S[:, 16 + c0:16 + c0 + w], scalar=1.0 / wdw,
                    in1=P[:, 16 + c0:16 + c0 + w], op0=ALU.mult, op1=ALU.subtract),
                    reads=[SB_[n], PB[n]], writes=[DB[n]])
                if n == 0:
                    sc.op("dve", lambda E, S=S, g=g, nfix=nfix: E.tensor_tensor(
                        out=small[:, 0:nfix], in0=S[:, 16:16 + nfix], in1=invc[:, g * 16:g * 16 + nfix],
                        op=ALU.mult), reads=[SB_[0], cB["invc"]], writes=[padB])
                    sc.op("dve", lambda E, nfix=nfix, P=P: E.tensor_tensor(
                        out=dbuf[:, 0:nfix], in0=small[:, 0:nfix], in1=P[:, 16:16 + nfix], op=ALU.subtract),
                        reads=[padB, PB[0], DB[0]], writes=[DB[0]])
            sc.op("act", lambda E, g=g, P=P: E.activation(
                out=opp_sb[:, g * 15:(g + 1) * 15], in_=P[:, 16 + SEQ - 15:16 + SEQ], func=AF.Copy),
                reads=[PB[4]], writes=[opsB])
            sc.op("act", lambda E, g=g, P=P: E.activation(
                out=ops4[:, g, :, 14:15], in_=P[:, 16 + SEQ:16 + T].rearrange("p (s o) -> p s o", o=1),
                func=AF.Copy), reads=[PB[4]], writes=[opsB])
        w_release()
        ev_pp = sc.dma("sp", o_pp, opp_sb[:], reads=[opsB], sem="out")
        ev_ps = sc.dma("sp", o_ps, ops_sb, reads=[opsB], sem="out")
        merge([mB[7][n] for n in range(5)], [opsB])
        merge(cXB + cCB, PbB[0] + BB + CB)
        merge([mB[slot][n] for slot in range(4, 7) for n in range(5)], PbB[1])

        for j in range(4):
            wt, wtB = w_acquire(f"in0_conv{j}")
            if j == 0:
                sc.op("dve", lambda E: E.memset(cbuf[:, 0:2], 0.0), writes=[cCB[0]])
            w0 = vecs[:, VC_CONVW + 0 * 4 + j:VC_CONVW + 0 * 4 + j + 1]
            w1 = vecs[:, VC_CONVW + 1 * 4 + j:VC_CONVW + 1 * 4 + j + 1]
            w2 = vecs[:, VC_CONVW + 2 * 4 + j:VC_CONVW + 2 * 4 + j + 1]
            for n in range(5):
                c0, w = TT[n]
                bx, bxB = nbank()
                bc, bcB = nbank()
                bb, bbB = nbank()
                proj(bx, bxB, wt, wtB, 0, n, wstride=384)
                proj(bc, bcB, wt, wtB, 128, n, wstride=384)
                proj(bb, bbB, wt, wtB, 256, n, wstride=384)
                if j == 0 and n == 1:
                    for nn in range(5):
                        pool_mm(3, nn)
                sc.op("act", lambda E, bx=bx, c0=c0, w=w: E.activation(
                    out=xsb[:, c0:c0 + w], in_=bx[:, 0:w], func=AF.Copy),
                    reads=[bxB], writes=[cXB[n]])
                sc.op("dve", lambda E, bc=bc, c0=c0, w=w: E.tensor_tensor(
                    out=cbuf[:, 2 + c0:2 + c0 + w], in0=bc[:, 0:w], in1=xsb[:, c0:c0 + w], op=ALU.mult),
                    reads=[bcB, cXB[n]], writes=[cCB[n]])
                yi = nxt("y", 2)
                yb = ybufs[yi]
                segs = [("p", c0, w)] if n < 4 else [("p", LASTP[0], LASTP[1]), ("s", SEQ, NS)]
                for kind, a0, aw in segs:
                    yo = a0 - c0
                    if kind == "p":
                        cm2 = cbuf[:, a0:a0 + aw]
                        cm1 = cbuf[:, 1 + a0:1 + a0 + aw]
                        crd = [cCB[n]] + ([cCB[n - 1]] if n > 0 else [])
                    else:
                        cm2 = hc4[:, j, :, 0:1].rearrange("p s o -> p (s o)")
                        cm1 = hc4[:, j, :, 1:2].rearrange("p s o -> p (s o)")
                        crd = [cB["hc"]]
                    sc.op("act", lambda E, cm2=cm2, aw=aw, yo=yo, w0=w0, yb=yb: E.activation(
                        out=yb[:, yo:yo + aw], in_=cm2, func=AF.Copy, scale=w0),
                        reads=crd + [cB["vecs"]], writes=[yB[yi]])
                    sc.op("dve", lambda E, cm1=cm1, aw=aw, yo=yo, w1=w1, yb=yb: E.scalar_tensor_tensor(
                        out=yb[:, yo:yo + aw], in0=cm1, scalar=w1, in1=yb[:, yo:yo + aw],
                        op0=ALU.mult, op1=ALU.add), reads=crd + [yB[yi]], writes=[yB[yi]])
                    sc.op("dve", lambda E, a0=a0, aw=aw, yo=yo, w2=w2, yb=yb: E.scalar_tensor_tensor(
                        out=yb[:, yo:yo + aw], in0=cbuf[:, 2 + a0:2 + a0 + aw], scalar=w2, in1=yb[:, yo:yo + aw],
                        op0=ALU.mult, op1=ALU.add), reads=[cCB[n], yB[yi]], writes=[yB[yi]])
                sc.op("dve", lambda E, bb=bb, c0=c0, w=w, j=j, yb=yb: E.tensor_tensor(
                    out=mix[:, 4 + j, c0:c0 + w], in0=bb[:, 0:w], in1=yb[:, 0:w], op=ALU.mult),
                    reads=[bbB, yB[yi]], writes=[mB[4 + j][n]])
                if n == 4:
                    sc.op("act", lambda E, j=j: E.activation(
                        out=ocp_sb[:, j * 2:(j + 1) * 2], in_=cbuf[:, 2 + SEQ - 2:2 + SEQ], func=AF.Copy),
                        reads=[cCB[4]], writes=[ocsB])
                if n == 4:
                    sc.op("act", lambda E, j=j: E.activation(
                        out=ocs4[:, j, :, 1:2], in_=cbuf[:, 2 + SEQ:2 + T].rearrange("p (s o) -> p s o", o=1),
                        func=AF.Copy), reads=[cCB[4]], writes=[ocsB])
            w_release()
        ev_cp = sc.dma("sp", o_cp, ocp_sb[:], reads=[ocsB], sem="out")
        ev_cs = sc.dma("sp", o_cs, ocs_sb[:], reads=[ocsB], sem="out")

        def run_outproj(prefix, gcol):
            done = []
            for n in out_proj(prefix):
                if done:
                    rms_b(done.pop(0), gcol)
                rms_a(n)
                done.append(n)
            pending[done[-1]] = (lambda k_=done[-1]: rms_b(k_, gcol))

        run_outproj("out0", VC_NFFN + 0)
        late_setup()
        ffn(0, lambda n: rms_tile(n, VC_NMIX + 8), fuse_tail=FUSE_L0)

        wv0, wv0B = w_acquire("v_0")
        wv1, wv1B = w_acquire("v_1")
        w_issue_special("u_0")

        def slotB(s_):
            return [mB[s_][n] for n in range(5)]
        ssB = Buf("ss")
        nhB = Buf("neghalf")
        sc.op("pool", lambda E: E.memset(small[:, 32:33], -0.5), writes=[nhB])
        sc.dma("sp", gv, d_gv, writes=slotB(1), sem="gvin")
        vN3 = vN.rearrange("p (i d) -> p i d", i=17)

        def tile_of(col):
            for n_, (c0_, w_) in enumerate(TT):
                if c0_ <= col < c0_ + w_:
                    return n_
            raise ValueError(col)
        for i in range(17):
            ntok = 128 if i < 16 else 16
            t0 = i * 128
            n = tile_of(t0)
            need_h(n)
            vt = vtmp2[i % 2]
            vtBs = slotB(6 + i % 2)
            for hh, (wv, wvB) in enumerate(((wv0, wv0B), (wv1, wv1B))):
                bk, bkB = nbank()

                def f(E, bk=bk, wv=wv, t0=t0, ntok=ntok):
                    last = None
                    for k in range(KC):
                        last = E.matmul(bk[0:ntok, :], lhsT=hT[:, k, t0:t0 + ntok], rhs=wv[:, k * 512:(k + 1) * 512],
                                        start=(k == 0), stop=(k == KC - 1))
                    return last
                wvBs = wvB if isinstance(wvB, list) else [wvB]
                sc.op("pe", f, reads=wvBs + [hB[k][n] for k in range(KC)], writes=[bkB])
                sc.op("act", lambda E, bk=bk, vt=vt, hh=hh, ntok=ntok: E.activation(
                    out=vt[0:ntok, hh * 512:(hh + 1) * 512], in_=bk[0:ntok, :], func=AF.Gelu),
                    reads=[bkB], writes=vtBs)
            sc.op("act", lambda E, vt=vt, i=i, ntok=ntok: E.activation(
                out=junk[0:ntok, :], in_=vt[0:ntok, :], func=AF.Square, accum_out=ssv[0:ntok, i:i + 1]),
                reads=vtBs, writes=slotB(0) + [ssB])
            sc.op("dve", lambda E, i=i, ntok=ntok: E.tensor_scalar(
                out=ssv[0:ntok, i:i + 1], in0=ssv[0:ntok, i:i + 1], scalar1=1.0 / D, scalar2=EPS,
                op0=ALU.mult, op1=ALU.add), reads=[ssB], writes=[ssB])
            sc.op("pool", lambda E, i=i, ntok=ntok: E.tensor_tensor(
                out=rsv[0:ntok, i:i + 1], in0=ssv[0:ntok, i:i + 1], in1=small[0:ntok, 32:33], op=ALU.pow),
                reads=[ssB, nhB], writes=[ssB])
            sc.op("dve", lambda E, vt=vt, i=i, ntok=ntok: E.scalar_tensor_tensor(
                out=vN3[0:ntok, i, :], in0=vt[0:ntok, :], scalar=rsv[0:ntok, i:i + 1], in1=gv[0:ntok, :],
                op0=ALU.mult, op1=ALU.mult), reads=vtBs + [ssB] + slotB(1),
                writes=[vNB[i]] + allR1)
            if i == 16:
                sc.op("dve", lambda E, vt=vt, i=i, ntok=ntok: E.scalar_tensor_tensor(
                    out=vout, in0=vt[0:ntok, :], scalar=rsv[0:ntok, i:i + 1], in1=gv[0:ntok, :],
                    op0=ALU.mult, op1=ALU.mult), reads=vtBs + [ssB] + slotB(1), writes=list(sqB[0:4]))
                ev_v = sc.dma("sp", o_v, vout, reads=list(sqB[0:4]), sem="out")
        w_release()

        for h in range(8):
            if h % 4 == 0:
                wt, wtB = w_acquire(f"u_{h // 4}")
                if DEBUG and h == 0:
                    sc.dma("pool", o_dbg, wt[:, 0:4096], reads=wtB, sem="dbg")
            for n in range(5):
                c0, w = TT[n]
                bu_, buB = nbank()
                proj(bu_, buB, wt, wtB, (h % 4) * 128, n, wstride=512)
                gi = nxt("gtmp", 2)
                sc.op("act", lambda E, bu_=bu_, gi=gi, w=w: E.activation(
                    out=gtmp[gi][:, 0:w], in_=bu_[:, 0:w], func=AF.Gelu),
                    reads=[buB], writes=[gtmpB[gi]])
                bm, bmB = nbank()
                pw = min(c0 + w, SEQ) - c0
                nq = pw // 128
                q0 = c0 // 128
                has_s = (c0 + w > SEQ)

                def f(E, bm=bm, h=h, pw=pw, nq=nq, q0=q0, has_s=has_s):
                    E.matmul(bm[:, 0:pw], lhsT=ones[:, :],
                             rhs=bs2[:, h * 128:(h + 1) * 128].unsqueeze(1).broadcast_to([128, nq, 128]),
                             start=True, stop=False)
                    if has_s:
                        E.matmul(bm[:, pw:pw + NS], lhsT=ones[:, :], rhs=bs02[:, h * 16:(h + 1) * 16],
                                 start=False, stop=False)
                    last = None
                    for q in range(nq):
                        last = E.matmul(bm[:, q * 128:(q + 1) * 128],
                                        lhsT=vN3[:, q0 + q, h * 128:(h + 1) * 128],
                                        rhs=wsT[:, h * 128:(h + 1) * 128], start=False,
                                        stop=(q == nq - 1 and not has_s))
                    if has_s:
                        last = E.matmul(bm[:, pw:pw + NS], lhsT=vN3[0:16, 16, h * 128:(h + 1) * 128],
                                        rhs=diag[0:16, h * 16:(h + 1) * 16], start=False, stop=True)
                    return last
                rd = [onesB, bsB, wsTB] + [vNB[q0 + q] for q in range(nq)]
                if has_s:
                    rd += [diagB, vNB[16]]
                sc.op("pe", f, reads=rd, writes=[bmB])
                sc.op("dve", lambda E, bm=bm, gi=gi, c0=c0, w=w, h=h: E.tensor_tensor(
                    out=mix[:, h, c0:c0 + w], in0=bm[:, 0:w], in1=gtmp[gi][:, 0:w], op=ALU.mult),
                    reads=[bmB, gtmpB[gi]], writes=[mB[h][n]])
            if h % 4 == 3 and f"u_{h // 4}" not in SPECIAL:
                w_release()

        act_warm_lnexp()
        run_outproj("out1", VC_NFFN + 8)

        out_evs = [ev_pp, ev_ps, ev_cp, ev_cs, ev_v]

        def final_tile(n):
            c0, w = TT[n]
            rms_tile(n, VC_NFIN, final=True)
            if n == 4:
                for hh, q_ in ((0, "sp"), (1, "act")):
                    out_evs.append(sc.dma(q_, o_y[:, xrange_(n, 4 * hh, 4 * hh + 4)], xT[:, xrange_(n, 4 * hh, 4 * hh + 4)],
                                          reads=[xB[c][n] for c in range(4 * hh, 4 * hh + 4)], sem=("out" if hh == 0 else "out2")))
                return
            out_evs.append(sc.dma("sp", o_y[:, xrange_(n)], xT[:, xrange_(n)],
                                  reads=[xB[c][n] for c in range(KC)], sem="out"))
        ffn(1, final_tile, fuse_tail=True)
        flush_pending()
        sc.wait("sp", [("out", sc.cnt["out"]), ("out2", sc.cnt["out2"])])
        sc.emit_all()
    return nc


_NC_CACHE = {}


def _get_program():
    if "nc" not in _NC_CACHE:
        _NC_CACHE["nc"] = build_program()
    return _NC_CACHE["nc"]


def _feat_major(a):
    n = a.shape[0]
    return a.T.reshape(KC, 128, n).transpose(1, 0, 2)


def kernel(x_prompt, x_sample, state_pool, state_conv, norm_mix, norm_ffn, norm_final,
           w_in_even, w_pool, pool_scale, conv_w, w_out_even,
           w_in_odd, norm_sg, w_s, b_s, w_out_odd,
           ffn_w_gate, ffn_w_up, ffn_w_down):
    inp = dict(x_prompt=x_prompt, x_sample=x_sample, state_pool=state_pool, state_conv=state_conv,
               norm_mix=norm_mix, norm_ffn=norm_ffn, norm_final=norm_final, w_in_even=w_in_even,
               w_pool=w_pool, pool_scale=pool_scale, conv_w=conv_w, w_out_even=w_out_even,
               w_in_odd=w_in_odd, norm_sg=norm_sg, w_s=w_s, b_s=b_s, w_out_odd=w_out_odd,
               ffn_w_gate=ffn_w_gate, ffn_w_up=ffn_w_up, ffn_w_down=ffn_w_down)
    inp = {k: np.asarray(v, dtype=np.float32) for k, v in inp.items()}
    nc = _get_program()

    wts = pack_weights(inp)
    vecs = pack_vecs(inp)
    gv_rep = np.ascontiguousarray(np.broadcast_to(inp["norm_sg"][0][None, :], (128, D)))
    wpool = np.ascontiguousarray(inp["w_pool"][0].transpose(1, 0, 2)).reshape(128, 512)
    wsT = np.ascontiguousarray(inp["w_s"][0].transpose(2, 0, 1)).reshape(128, 1024)
    ws00 = np.ascontiguousarray(np.broadcast_to(inp["w_s"][0][:, 0, 0][None, :], (16, 8)))
    bs = np.ascontiguousarray(np.broadcast_to(inp["b_s"][0].reshape(1, 1024), (2, 1024)))
    bs0 = np.ascontiguousarray(np.broadcast_to(np.repeat(inp["b_s"][0][:, 0], 16).reshape(1, 128), (2, 128)))
    sel = np.eye(2, dtype=np.float32)
    invc = np.empty((128, 4, 16), np.float32)
    for g, w in enumerate(WINS):
        invc[:, g, :] = 1.0 / np.minimum(np.arange(16) + 1, w).astype(np.float32)
    invc = invc.reshape(128, 64)

    in_maps = []
    for c in range(NCORES):
        xs = inp["x_sample"][c * NS:(c + 1) * NS, 0, :]
        xfm = np.concatenate([_feat_major(inp["x_prompt"][c]), _feat_major(xs)], axis=2)
        xT = np.ascontiguousarray(np.concatenate(
            [xfm[:, :, c0_:c0_ + w_].reshape(128, KC * w_) for (c0_, w_) in TT], axis=1))
        hp = np.ascontiguousarray(inp["state_pool"][0, c * NS:(c + 1) * NS].reshape(NS, 15, 4, 128)
                                  .transpose(3, 2, 0, 1)).reshape(128, 960)
        hc = np.ascontiguousarray(inp["state_conv"][0, c * NS:(c + 1) * NS].reshape(NS, 2, 4, 128)
                                  .transpose(3, 2, 0, 1)).reshape(128, 128)
        in_maps.append({"xT": xT, "wts": wts, "vecs": vecs, "gv_rep": gv_rep, "wpool": wpool,
                        "wsT": wsT, "ws00": ws00, "bs": bs, "bs0": bs0, "sel": sel, "invc": invc,
                        "hp": hp, "hc": hc})
    res = run_bass_kernel_spmd(nc, in_maps, core_ids=list(range(NCORES)))
    R = res.results

    y_prompt = np.empty((NCORES, SEQ, D), np.float32)
    y_sample = np.empty((NCORES * NS, 1, D), np.float32)
    pool_p = np.empty((1, NCORES, 15, 512), np.float32)
    pool_s = np.empty((1, NCORES * NS, 15, 512), np.float32)
    conv_p = np.empty((1, NCORES, 2, 512), np.float32)
    conv_s = np.empty((1, NCORES * NS, 2, 512), np.float32)
    v_s = np.empty((1, NCORES * NS, 1, D), np.float32)
    for c in range(NCORES):
        r = R[c]
        yflat = np.asarray(r["yT"]).reshape(128, KC * T)
        yT = np.empty((128, KC, T), np.float32)
        for n_, (c0_, w_) in enumerate(TT):
            yT[:, :, c0_:c0_ + w_] = yflat[:, XOFF[n_]:XOFF[n_] + KC * w_].reshape(128, KC, w_)
        y_prompt[c] = yT[:, :, :SEQ].transpose(2, 1, 0).reshape(SEQ, D)
        y_sample[c * NS:(c + 1) * NS, 0] = yT[:, :, SEQ:].transpose(2, 1, 0).reshape(NS, D)
        pool_p[0, c] = np.asarray(r["o_pp"]).reshape(128, 4, 15).transpose(2, 1, 0).reshape(15, 512)
        pool_s[0, c * NS:(c + 1) * NS] = (np.asarray(r["o_ps"]).reshape(128, 4, NS, 15)
                                          .transpose(2, 3, 1, 0).reshape(NS, 15, 512))
        conv_p[0, c] = np.asarray(r["o_cp"]).reshape(128, 4, 2).transpose(2, 1, 0).reshape(2, 512)
        conv_s[0, c * NS:(c + 1) * NS] = (np.asarray(r["o_cs"]).reshape(128, 4, NS, 2)
                                          .transpose(2, 3, 1, 0).reshape(NS, 2, 512))
        v_s[0, c * NS:(c + 1) * NS, 0] = np.asarray(r["o_v"]).reshape(NS, D)
    return (y_prompt, y_sample, pool_p, pool_s, conv_p, conv_s, v_s)
```
